# Optimizing a Trainium2 kernel written in Bass

```python
import math
import jax, jax.numpy as jnp
from jax import lax
import numpy as np

D_MODEL = 1024
BATCH = 8
SEQ = 2048
DEPTH = 4

GRID_W = 64
CTX_LEN = 256
MIX = 256
N_BRANCH = 4
RET_HEADS = 4
RET_HEAD_DIM = MIX // RET_HEADS
CHUNK = 128
POOL_WINDOWS = (2, 4, 8, 16)
POOL_GROUP = MIX // len(POOL_WINDOWS)
FFT_GROUPS = 4
FFT_GROUP_DIM = MIX // FFT_GROUPS
D_FF = -(-8 * D_MODEL // (3 * 256)) * 256
IN_WIDTH = 9 * MIX + N_BRANCH * D_MODEL
ROPE_BASE = 10000.0
EPS = 1e-6

kernel_name = 'hybrid_retention_conv_fourier_pool_dit_block'


def rmsnorm(x, g):
    xf = x.astype(jnp.float32)
    y = xf * lax.rsqrt(jnp.mean(xf * xf, axis=-1, keepdims=True) + EPS)
    return (y * g.astype(jnp.float32)).astype(x.dtype)


def rope_1d(x, pos):
    half = x.shape[-1] // 2
    freqs = ROPE_BASE ** (-jnp.arange(half, dtype=jnp.float32) / half)
    ang = pos[:, None] * freqs[None, :]
    cos, sin = jnp.cos(ang), jnp.sin(ang)
    x1, x2 = x[..., :half], x[..., half:]
    return jnp.concatenate([x1 * cos - x2 * sin, x1 * sin + x2 * cos], axis=-1)


def rope_2d(x, rows, cols):
    half = x.shape[-1] // 2
    return jnp.concatenate([rope_1d(x[..., :half], rows), rope_1d(x[..., half:], cols)], axis=-1)


def heads(t):
    b, n, _ = t.shape
    return t.reshape(b, n, RET_HEADS, RET_HEAD_DIM).transpose(0, 2, 1, 3).astype(jnp.float32)


def retention_direction(q, k, v, log_g, s0, strict):
    b, h, n, dk = q.shape
    dv = v.shape[-1]
    nc = n // CHUNK
    idx = jnp.arange(CHUNK, dtype=jnp.float32)
    diff = idx[:, None] - idx[None, :]
    mask = (diff > 0) if strict else (diff >= 0)
    intra = jnp.where(mask[None], jnp.exp(log_g[:, None, None] * jnp.maximum(diff, 0.0)[None]), 0.0)
    q_decay = jnp.exp(log_g[:, None] * (idx + 1.0)[None])[None, :, :, None]
    k_decay = jnp.exp(log_g[:, None] * (CHUNK - 1.0 - idx)[None])[None, :, :, None]
    chunk_decay = jnp.exp(log_g * CHUNK)[None, :, None, None]

    def to_chunks(t):
        return jnp.moveaxis(t.reshape(b, h, nc, CHUNK, t.shape[-1]), 2, 0)

    def step(state, inp):
        qc, kc, vc = inp
        scores = jnp.einsum('bhid,bhjd->bhij', qc, kc) * intra[None]
        out = (jnp.einsum('bhij,bhjv->bhiv', scores, vc)
               + jnp.einsum('bhid,bhdv->bhiv', qc, state) * q_decay)
        state = state * chunk_decay + jnp.einsum('bhjd,bhjv->bhdv', kc * k_decay, vc)
        return state, out

    state, ys = lax.scan(step, s0, (to_chunks(q), to_chunks(k), to_chunks(v)))
    return jnp.moveaxis(ys, 0, 2).reshape(b, h, n, dv), state


def bidirectional_retention(q, k, v, log_g, s_fwd0, s_bwd0):
    y_f, s_f = retention_direction(q, k, v, log_g[0], s_fwd0, False)
    flip = lambda t: t[:, :, ::-1]
    y_b, s_b = retention_direction(flip(q), flip(k), flip(v), log_g[1], s_bwd0, True)
    return y_f + flip(y_b), s_f, s_b


def retention_output(y, g):
    mu = jnp.mean(y, axis=-1, keepdims=True)
    var = jnp.mean(jnp.square(y - mu), axis=-1, keepdims=True)
    y = (y - mu) * lax.rsqrt(var + EPS)
    b, h, n, d = y.shape
    y = y.transpose(0, 2, 1, 3).reshape(b, n, h * d)
    return (y * jax.nn.silu(g.astype(jnp.float32))).astype(g.dtype)


def short_conv(p, conv_w):
    bg, cg, xv = p[..., 4 * MIX:5 * MIX], p[..., 5 * MIX:6 * MIX], p[..., 6 * MIX:7 * MIX]
    u = jnp.pad(cg * xv, ((0, 0), (1, 1), (0, 0)))
    conv = u[:, :-2] * conv_w[0] + u[:, 1:-1] * conv_w[1] + u[:, 2:] * conv_w[2]
    return bg * conv


def fourier_mix(u):
    b, n, _ = u.shape
    z = jnp.fft.fft2(u.astype(jnp.float32).reshape(b, n, FFT_GROUPS, FFT_GROUP_DIM), axes=(1, 3), norm='ortho')
    return jnp.real(z).reshape(b, n, MIX).astype(u.dtype)


def multiscale_pool(u, pool_w, pool_scale):
    b, n, _ = u.shape
    uf = u.astype(jnp.float32)
    csum = jnp.concatenate([jnp.zeros((b, 1, MIX), jnp.float32), jnp.cumsum(uf, axis=1)], axis=1)
    t = jnp.arange(n)
    outs = []
    for gi, w in enumerate(POOL_WINDOWS):
        lo = jnp.clip(t - w // 2, 0, n)
        hi = jnp.clip(t - w // 2 + w, 0, n)
        sl = slice(gi * POOL_GROUP, (gi + 1) * POOL_GROUP)
        cs = csum[:, :, sl]
        mean = (cs[:, hi] - cs[:, lo]) / (hi - lo).astype(jnp.float32)[None, :, None]
        outs.append(mean - uf[:, :, sl])
    pooled = jnp.stack(outs, axis=2)
    mixed = jnp.einsum('bngi,gio->bngo', pooled, pool_w.astype(jnp.float32)).reshape(b, n, MIX)
    return (mixed * pool_scale.astype(jnp.float32)).astype(u.dtype)


def mixer_output(p, ret_y, conv_w, pool_w, pool_scale, w_branch, w_o):
    branches = (retention_output(ret_y, p[..., 3 * MIX:4 * MIX]),
                short_conv(p, conv_w),
                fourier_mix(p[..., 7 * MIX:8 * MIX]),
                multiscale_pool(p[..., 8 * MIX:9 * MIX], pool_w, pool_scale))
    gates = jax.nn.sigmoid(p[..., 9 * MIX:])
    merged = 0.0
    for i, br in enumerate(branches):
        merged = merged + gates[..., i * D_MODEL:(i + 1) * D_MODEL] * (br @ w_branch[i])
    return merged @ w_o


def swiglu(h, w_up, w_down):
    a, u = jnp.split(h @ w_up, 2, axis=-1)
    return (jax.nn.silu(a) * u) @ w_down


def setup_inputs(seed: int = 0) -> dict:
    key = jax.random.key(seed)
    ks = jax.random.split(key, 16)
    nrm = lambda k, shape, s: jax.random.normal(k, shape, jnp.float32) * s
    base_decay = jnp.asarray(np.log(2.0 ** (5 + np.arange(RET_HEADS)) - 1.0).astype(np.float32))
    return {
        'x': nrm(ks[0], (BATCH, SEQ, D_MODEL), 1.0),
        'c': nrm(ks[1], (BATCH, D_MODEL), 1.0),
        'ctx': nrm(ks[2], (BATCH, CTX_LEN, D_MODEL), 1.0),
        'c_ctx': nrm(ks[3], (D_MODEL,), 1.0),
        'w_mod': nrm(ks[4], (DEPTH, D_MODEL, 6 * D_MODEL), 0.5 * D_MODEL ** -0.5),
        'b_mod': nrm(ks[5], (DEPTH, 6 * D_MODEL), 0.02),
        'norm_g': 1.0 + nrm(ks[6], (DEPTH, 4, D_MODEL), 0.02),
        'w_in': nrm(ks[7], (DEPTH, D_MODEL, IN_WIDTH), D_MODEL ** -0.5),
        'ret_decay': base_decay[None, None, :] + nrm(ks[8], (DEPTH, 2, RET_HEADS), 0.1),
        'conv_w': nrm(ks[9], (DEPTH, 3, MIX), 3 ** -0.5),
        'pool_w': nrm(ks[10], (DEPTH, len(POOL_WINDOWS), POOL_GROUP, POOL_GROUP), POOL_GROUP ** -0.5),
        'pool_scale': 1.0 + nrm(ks[11], (DEPTH, MIX), 0.02),
        'w_branch': nrm(ks[12], (DEPTH, N_BRANCH, MIX, D_MODEL), MIX ** -0.5),
        'w_o': nrm(ks[13], (DEPTH, D_MODEL, D_MODEL), D_MODEL ** -0.5),
        'ffn_w_up': nrm(ks[14], (DEPTH, D_MODEL, 2 * D_FF), D_MODEL ** -0.5),
        'ffn_w_down': nrm(ks[15], (DEPTH, D_FF, D_MODEL), D_FF ** -0.5),
    }


def reference(x, c, ctx, c_ctx, w_mod, b_mod, norm_g, w_in, ret_decay, conv_w, pool_w,
              pool_scale, w_branch, w_o, ffn_w_up, ffn_w_down):
    n = x.shape[1]
    ROWS = n // GRID_W
    rows = jnp.repeat(jnp.arange(ROWS, dtype=jnp.float32), GRID_W)
    cols = jnp.tile(jnp.arange(GRID_W, dtype=jnp.float32), ROWS)
    silu_c = jax.nn.silu(c)
    silu_cc = jax.nn.silu(c_ctx)
    scale_q = RET_HEAD_DIM ** -0.5
    for l in range(DEPTH):
        need_ctx = l < DEPTH - 1
        mod_l = (silu_c @ w_mod[l] + b_mod[l])[:, None, :]
        mod_c = (silu_cc @ w_mod[l] + b_mod[l])[None, None, :]
        sh1, sc1, g1, sh2, sc2, g2 = jnp.split(mod_l, 6, axis=-1)
        csh1, csc1, cg1, csh2, csc2, cg2 = jnp.split(mod_c, 6, axis=-1)

        h_lat = rmsnorm(x, norm_g[l, 0]) * (1.0 + sc1) + sh1
        h_ctx = rmsnorm(ctx, norm_g[l, 0]) * (1.0 + csc1) + csh1
        p_lat = h_lat @ w_in[l]
        p_ctx = h_ctx @ w_in[l]

        log_g = jax.nn.log_sigmoid(ret_decay[l].astype(jnp.float32))
        q_l = rope_2d(heads(p_lat[..., 0:MIX]) * scale_q, rows, cols)
        k_l = rope_2d(heads(p_lat[..., MIX:2 * MIX]), rows, cols)
        v_l = heads(p_lat[..., 2 * MIX:3 * MIX])
        q_c = heads(p_ctx[..., 0:MIX]) * scale_q
        k_c = heads(p_ctx[..., MIX:2 * MIX])
        v_c = heads(p_ctx[..., 2 * MIX:3 * MIX])
        s0 = jnp.zeros((x.shape[0], RET_HEADS, RET_HEAD_DIM, RET_HEAD_DIM), jnp.float32)
        y_c, s_f, s_b = bidirectional_retention(q_c, k_c, v_c, log_g, s0, s0)
        y_l, _, _ = bidirectional_retention(q_l, k_l, v_l, log_g, s_f, s_b)

        mix_lat = mixer_output(p_lat, y_l, conv_w[l], pool_w[l], pool_scale[l], w_branch[l], w_o[l])
        x = x + g1 * rmsnorm(mix_lat, norm_g[l, 1])
        h = rmsnorm(x, norm_g[l, 2]) * (1.0 + sc2) + sh2
        x = x + g2 * rmsnorm(swiglu(h, ffn_w_up[l], ffn_w_down[l]), norm_g[l, 3])

        if need_ctx:
            mix_ctx = mixer_output(p_ctx, y_c, conv_w[l], pool_w[l], pool_scale[l], w_branch[l], w_o[l])
            ctx = ctx + cg1 * rmsnorm(mix_ctx, norm_g[l, 1])
            hc = rmsnorm(ctx, norm_g[l, 2]) * (1.0 + csc2) + csh2
            ctx = ctx + cg2 * rmsnorm(swiglu(hc, ffn_w_up[l], ffn_w_down[l]), norm_g[l, 3])
    return x
```

```python
import numpy as np
import ml_dtypes
import concourse.bass as bass
import concourse.mybir as mybir
from concourse.bass_utils import run_bass_kernel_spmd

F32 = mybir.dt.float32
BF16 = mybir.dt.bfloat16
ALU = mybir.AluOpType
AF = mybir.ActivationFunctionType
AX = mybir.AxisListType

D = 1024
KC = 8
SEQ = 2048
CTX = 256
NTOK = SEQ + CTX
NCH = NTOK // 128
DEPTH = 4
MIXW = 2304
DFF = 2816
EPS = 1e-6
TILES = [(0, 512, 0), (512, 512, 0), (1024, 512, 0), (1536, 512, 0), (2048, 256, 1)]
SUPER = [[(0, 512, 0), (512, 256, 0)], [(768, 512, 0), (1280, 256, 0)], [(1536, 512, 0), (2048, 256, 1)]]


class _Op:
    __slots__ = ("eng", "sem", "count", "dma", "nosig")


class Prog:
    NPOOL = 12

    def __init__(self, nc, n_epochs=1):
        self.nc = nc
        self.engs = {"pe": nc.tensor, "act": nc.scalar, "dve": nc.vector,
                     "pool": nc.gpsimd, "sp": nc.sync}
        self.esem = {e: [nc.alloc_semaphore(name=f"s_{e}_{i}") for i in range(n_epochs)]
                     for e in self.engs}
        self.ecnt = {e: [0] * n_epochs for e in self.engs}
        self.dsem = {e: [nc.alloc_semaphore(name=f"d_{e}_{i}") for i in range(self.NPOOL)]
                     for e in ("sp", "pool")}
        self.dcnt = {e: [0] * self.NPOOL for e in self.dsem}
        self.dlast = {e: [None] * self.NPOOL for e in self.dsem}
        self.dnext = {e: 0 for e in self.dsem}
        self.epoch = 0
        self.last_writer = {}
        self.readers = {}
        self.known = {e: {} for e in self.engs}
        self.last_op = {e: None for e in self.engs}
        self.pending_pe = []
        self.nops = 0
        self.nwaits = 0

    def set_epoch(self, i):
        self.epoch = i

    def _wait(self, eng, sem, val):
        kn = self.known[eng]
        key = id(sem)
        if kn.get(key, 0) >= val:
            return
        kn[key] = val
        self.engs[eng].wait_ge(sem, val)
        self.nwaits += 1

    def add(self, eng, emit, reads=(), writes=(), dma=False, nosig=False):
        deps = []
        for k in reads:
            w = self.last_writer.get(k)
            if w is not None:
                deps.append(w)
        for k in writes:
            w = self.last_writer.get(k)
            if w is not None:
                deps.append(w)
            deps.extend(self.readers.get(k, ()))
        for d in deps:
            if (not d.dma) and (not dma) and d.eng == "pe" and eng == "pe":
                continue
            if d.sem is None:
                raise RuntimeError("dependency on uncovered nosig PE op")
            self._wait(eng, d.sem, d.count)
        op = _Op()
        op.eng = eng
        op.dma = dma
        op.nosig = nosig
        op.sem = None
        op.count = 0
        if dma:
            i = self.dnext[eng]
            self.dnext[eng] = (i + 1) % self.NPOOL
            sem = self.dsem[eng][i]
            if self.dcnt[eng][i] > 0:
                self._wait(eng, sem, self.dcnt[eng][i])
            self.dcnt[eng][i] += 16
            op.sem = sem
            op.count = self.dcnt[eng][i]
            self.dlast[eng][i] = op
            ins = emit(self.engs[eng])
            ins.then_inc(sem, 16)
        else:
            ins = emit(self.engs[eng])
            if nosig:
                self.pending_pe.append(op)
            else:
                self.ecnt[eng][self.epoch] += 1
                op.sem = self.esem[eng][self.epoch]
                op.count = self.ecnt[eng][self.epoch]
                ins.then_inc(op.sem, 1)
                if eng == "pe" and self.pending_pe:
                    for p in self.pending_pe:
                        p.sem = op.sem
                        p.count = op.count
                    self.pending_pe = []
            self.last_op[eng] = op
        for k in reads:
            self.readers.setdefault(k, []).append(op)
        for k in writes:
            self.last_writer[k] = op
            self.readers[k] = []
        self.nops += 1
        return op

    def barrier(self):
        assert not self.pending_pe
        ops = [o for o in self.last_op.values() if o is not None]
        for e in self.dlast:
            ops.extend(o for o in self.dlast[e] if o is not None)
        for eng in self.engs:
            for d in ops:
                self._wait(eng, d.sem, d.count)

    def finish(self, eng="sp"):
        for e in self.dlast:
            for o in self.dlast[e]:
                if o is not None:
                    self._wait(eng, o.sem, o.count)


def make_consts():
    bf = ml_dtypes.bfloat16
    c = {}
    c["c_ident"] = np.eye(128, dtype=np.float32).astype(bf)
    c["c_onesdiv"] = np.full((128, 128), 1.0 / 1024, np.float32).astype(bf)
    e = np.arange(128) % 64
    fi = (e % 16).astype(np.float64)
    freq = 10000.0 ** (-fi / 16.0)
    t = np.arange(SEQ)
    rows = (t // 64).astype(np.float64)
    cols = (t % 64).astype(np.float64)
    pos = np.where((e < 32)[:, None], rows[None, :], cols[None, :])
    ang = pos * freq[:, None]
    x1 = (e % 32) < 16
    c["c_cos"] = np.cos(ang).astype(np.float32).astype(bf)
    c["c_sin"] = np.where(x1[:, None], -np.sin(ang), np.sin(ang)).astype(np.float32).astype(bf)
    partner = np.where(x1, np.arange(128) + 16, np.arange(128) - 16)
    rp = np.zeros((128, 128), np.float32)
    rp[partner, np.arange(128)] = 1.0
    c["c_ropeP"] = rp.astype(bf)
    j = np.arange(128)[:, None].astype(np.float32)
    i = np.arange(128)[None, :].astype(np.float32)
    c["c_pm"] = np.maximum(i - j, 0).astype(np.float32)
    c["c_nm"] = np.maximum(j - i, 0).astype(np.float32)
    c["c_iota1"] = np.broadcast_to(i + 1.0, (128, 128)).astype(np.float32).copy()
    c["c_iotar"] = np.broadcast_to(128.0 - i, (128, 128)).astype(np.float32).copy()
    pidx = np.arange(128)
    hm = np.zeros((128, 2, 128), np.float32)
    for hh in range(2):
        hm[pidx // 64 == hh, hh, :] = 1.0
    c["c_hmask"] = hm.astype(bf)
    c["c_bmask"] = (pidx[:, None] // 64 == pidx[None, :] // 64).astype(np.float32)
    c["c_jcols"] = np.stack([127.0 - np.arange(128), np.arange(128) * 1.0], axis=1).astype(np.float32)
    cs = np.zeros((128, 256), np.float64)
    jj = np.arange(64)[:, None]
    mm = np.arange(64)[None, :]
    a = 2 * np.pi * ((jj * mm) % 64) / 64.0
    for g in range(2):
        cs[g * 64:(g + 1) * 64, g * 64:(g + 1) * 64] = np.cos(a)
        cs[g * 64:(g + 1) * 64, 128 + g * 64:128 + (g + 1) * 64] = np.sin(a)
    c["c_cs64"] = cs.astype(np.float32).astype(bf)
    n = np.arange(SEQ)
    a = 2 * np.pi * ((n[:, None] * n[None, :]) % SEQ) / float(SEQ)
    c["c_dftc"] = np.cos(a).astype(np.float32).astype(bf)
    c["c_dfts"] = (-np.sin(a)).astype(np.float32).astype(bf)
    n = np.arange(CTX)
    a = 2 * np.pi * ((n[:, None] * n[None, :]) % CTX) / float(CTX)
    cc = np.cos(a).astype(np.float32).reshape(2, 128, CTX).transpose(1, 0, 2)
    sc = (-np.sin(a)).astype(np.float32).reshape(2, 128, CTX).transpose(1, 0, 2)
    c["c_dftctx"] = np.ascontiguousarray(np.stack([cc, sc], axis=1)).astype(bf)
    rc = np.zeros((4, 16), np.float32)
    nn = SEQ
    tt = np.arange(nn)
    for g, w in enumerate((2, 4, 8, 16)):
        lo = np.clip(tt - w // 2, 0, nn)
        hi = np.clip(tt - w // 2 + w, 0, nn)
        r = 1.0 / (hi - lo).astype(np.float32)
        rc[g, 0:8] = r[0:8]
        rc[g, 8:16] = r[nn - 8:nn]
    pe = np.zeros((128, 2, 16), np.float32)
    for p in range(128):
        for ch in range(2):
            pe[p, ch] = rc[ch * 2 + p // 64]
    c["c_pedge"] = pe
    return c


CONST_SHAPES = {
    "c_ident": ([128, 128], BF16), "c_onesdiv": ([128, 128], BF16), "c_ropeP": ([128, 128], BF16), "c_pm": ([128, 128], F32),
    "c_nm": ([128, 128], F32), "c_iota1": ([128, 128], F32), "c_iotar": ([128, 128], F32),
    "c_jcols": ([128, 2], F32), "c_cs64": ([128, 256], BF16), "c_dftctx": ([128, 2, 2, CTX], BF16),
    "c_pedge": ([128, 2, 16], F32), "c_hmask": ([128, 2, 128], BF16), "c_bmask": ([128, 128], F32),
}


class _Stop(Exception):
    pass


def build(depth=DEPTH, dbg=None, stop=None):
    nc = bass.Bass("TRN2", target_bir_lowering=False)

    def din(name, shape, dt=F32):
        return nc.dram_tensor(name, list(shape), dt, kind="ExternalInput").ap()

    xin = din("xin", [128, KC, NTOK])
    c2 = din("c2", [128, KC, 2])
    w_mod = din("w_mod", [DEPTH, D, 6 * D])
    b_mod = din("b_mod", [DEPTH, 128, 48])
    norm_g = din("norm_g", [DEPTH, 128, 4, KC])
    w_in = din("w_in", [DEPTH, D, 6400])
    ret_decay = din("ret_decay", [DEPTH, 8])
    conv_w = din("conv_w", [DEPTH, 128, 2, 3])
    pool_w = din("pool_w", [DEPTH, 4, 64, 64])
    pool_scale = din("pool_scale", [DEPTH, 128, 2])
    w_branch = din("w_branch", [DEPTH, 4, 256, D])
    w_o = din("w_o", [DEPTH, D, D])
    w_up = din("ffn_w_up", [DEPTH, D, 2 * DFF])
    w_down = din("ffn_w_down", [DEPTH, DFF, D])
    cd = {k: din(k, sh, dt) for k, (sh, dt) in CONST_SHAPES.items()}
    cos_d = din("c_cos", [128, SEQ], BF16)
    sin_d = din("c_sin", [128, SEQ], BF16)
    dftc = din("c_dftc", [SEQ, SEQ], BF16)
    dfts = din("c_dfts", [SEQ, SEQ], BF16)
    yout = nc.dram_tensor("yout", [128, KC, SEQ], F32, kind="ExternalOutput").ap()
    h_scr = nc.dram_tensor("h_scr", [128, KC, NTOK], BF16).ap()
    br_scr = nc.dram_tensor("br_scr", [128, 8, NTOK], BF16).ap()
    dbg_out = None
    if dbg is not None:
        dbg_out = nc.dram_tensor("dbg", list(dbg[1]), dbg[2], kind="ExternalOutput").ap()

    P = Prog(nc, n_epochs=depth)
    from contextlib import ExitStack
    top = ExitStack()

    uid = [0]

    def chk(tag):
        if stop == tag:
            raise _Stop()

    def sb(name, shape, dt, st=None):
        uid[0] += 1
        return (st or top).enter_context(nc.sbuf_tensor(f"{name}_u{uid[0]}", list(shape), dt))

    def MM(out, lhsT, rhs, start, stop, r, w):
        P.add("pe", lambda e: e.matmul(out, lhsT, rhs, start=start, stop=stop), reads=r, writes=w,
              nosig=not stop)

    def TR(out, in_, r, w):
        P.add("pe", lambda e: e.transpose(out, in_, ident[:]), reads=list(r) + ["c_ident"], writes=w)

    def ACT(out, in_, func, r, w, bias=None, scale=None):
        kw = {}
        if bias is not None:
            kw["bias"] = bias
        if scale is not None:
            kw["scale"] = scale
        P.add("act", lambda e: e.activation(out, in_, func, **kw), reads=r, writes=w)

    def TT(eng, out, in0, in1, op, r, w):
        P.add(eng, lambda e: e.tensor_tensor(out, in0, in1, op), reads=r, writes=w)

    def TS(eng, out, in0, s1, s2, op0, op1, r, w):
        if op1 is None:
            P.add(eng, lambda e: e.tensor_scalar(out, in0, s1, None, op0=op0), reads=r, writes=w)
        else:
            P.add(eng, lambda e: e.tensor_scalar(out, in0, s1, s2, op0=op0, op1=op1), reads=r, writes=w)

    def STT(eng, out, in0, scalar, in1, op0, op1, r, w):
        P.add(eng, lambda e: e.scalar_tensor_tensor(out, in0, scalar, in1, op0=op0, op1=op1), reads=r, writes=w)

    def CP(eng, out, in_, r, w):
        if eng == "act":
            P.add("act", lambda e: e.copy(out, in_), reads=r, writes=w)
        else:
            P.add(eng, lambda e: e.tensor_copy(out, in_), reads=r, writes=w)

    def MEMSET(eng, ap, val, w):
        P.add(eng, lambda e: e.memset(ap, val), writes=w)

    def DMA(q, out, in_, r, w):
        P.add(q, lambda e: e.dma_start(out=out, in_=in_), reads=r, writes=w, dma=True)

    class Ring:
        def __init__(self, name, shape, dt, n, st=None):
            self.t = [sb(f"{name}{i}", shape, dt, st) for i in range(n)]
            self.k = [f"{name}{i}" for i in range(n)]
            self.i = 0
            self.n = n

        def next(self):
            i = self.i
            self.i = (i + 1) % self.n
            return self.t[i], self.k[i]

    x_all = sb("x_all", [128, KC, NTOK], F32)
    cst = {k: sb("s" + k, sh, dt) for k, (sh, dt) in CONST_SHAPES.items()}
    ident = cst["c_ident"]
    onesdiv = cst["c_onesdiv"]
    epsc = sb("epsc", [128, 1], F32)
    cc2 = sb("cc2", [128, KC, 2], F32)
    scT = sb("scT", [128, KC, 2], BF16)
    modv = [sb(f"modv{i}", [128, 48, 2], F32) for i in range(2)]
    bmod = [sb(f"bmod{i}", [128, 48], F32) for i in range(2)]
    gnorm = [sb(f"gnorm{i}", [128, 4, KC], F32) for i in range(2)]
    A1 = [sb(f"A1_{i}", [128, KC, 2], F32) for i in range(2)]
    B1 = [sb(f"B1_{i}", [128, KC, 2], F32) for i in range(2)]
    A2 = [sb(f"A2_{i}", [128, KC, 2], F32) for i in range(2)]
    B2 = [sb(f"B2_{i}", [128, KC, 2], F32) for i in range(2)]
    cw = sb("cw", [128, 2, 3], F32)
    pscale = sb("pscale", [128, 2], F32)
    pwbd = sb("pwbd", [128, 2, 128], BF16)
    rd = sb("rdec", [128, 8], F32)
    lg = sb("lg", [128, 8], F32)
    lgsel = sb("lgsel", [128, 4], F32)
    m_all = sb("m_all", [128, 4, 128], BF16)
    kdt = sb("kdt", [128, 8], F32)
    qdt = sb("qdt", [128, 2, 2, 128], BF16)
    cdt = sb("cdt", [128, 4], F32)
    NSLOT = 3
    SLOTN = 4096
    wbig = sb("wbig", [128, NSLOT * SLOTN], BF16)
    wslot = [wbig[:, i * SLOTN:(i + 1) * SLOTN] for i in range(NSLOT)]
    wsl_i = [0]

    def next_slot():
        i = wsl_i[0]
        wsl_i[0] = (i + 1) % NSLOT
        return wslot[i], f"wslot{i}"

    ps = [top.enter_context(nc.psum_tensor(f"ps{i}", [128, 512], F32)) for i in range(8)]
    ps_i = [0]

    def PS():
        i = ps_i[0]
        ps_i[0] = (i + 1) % 8
        return ps[i], ("ps", i)

    tf = Ring("tf", [128, 512], F32, 3)
    tb = Ring("tb", [128, 512], BF16, 3)
    rst = Ring("rst", [128, 512], F32, 2)

    for k in CONST_SHAPES:
        DMA("sp", cst[k][:], cd[k], [], [k])
    MEMSET("dve", epsc[:], EPS, ["epsc"])
    MEMSET("dve", pwbd[:], 0.0, [("pwbd", g) for g in range(4)])
    DMA("sp", cc2[:], c2, [], ["cc2"])
    for (t0, T, s) in TILES:
        DMA("sp", x_all[:, :, t0:t0 + T], xin[:, :, t0:t0 + T], [], [("x", t0)])
    ACT(scT[:], cc2[:], AF.Silu, ["cc2"], ["scT"])

    def xkeys(t0, T):
        return [("x", t0)] if (t0 % 512 == 0 and (T == 512 or t0 == 2048)) else \
            [("x", a) for a in sorted({(t0 // 512) * 512, ((t0 + T - 1) // 512) * 512})]

    def emit_mod(l):
        pb = l % 2
        mv = modv[pb]
        DMA("sp", bmod[pb][:], b_mod[l], [], [f"bmod{pb}"])
        DMA("sp", gnorm[pb][:], norm_g[l], [], [f"gnorm{pb}"])
        pm, pmk = PS()
        wv = w_mod[l].rearrange("(k p) n -> p k n", p=128)
        for piece in range(12):
            ws, wk = next_slot()
            wsv = ws[:, 0:KC * 512].rearrange("p (k n) -> p k n", k=KC)
            DMA("pool", wsv, wv[:, :, piece * 512:(piece + 1) * 512], [], [wk])
            for jj in range(4):
                j = piece * 4 + jj
                for k in range(KC):
                    MM(pm[:, 2 * j:2 * j + 2], wsv[:, k, jj * 128:(jj + 1) * 128], scT[:, k, :],
                       k == 0, k == KC - 1, [wk, "scT"], [pmk])
        TT("dve", mv[:], pm[:, 0:96].rearrange("p (j s) -> p j s", s=2),
           bmod[pb][:].unsqueeze(2).broadcast_to([128, 48, 2]), ALU.add, [pmk, f"bmod{pb}"], [f"modv{pb}"])
        g = gnorm[pb]
        for (dst, nm, sc_lo, gi, plus1) in ((A1[pb], "A1", 8, 0, True), (B1[pb], "B1", 16, 1, False),
                                            (A2[pb], "A2", 32, 2, True), (B2[pb], "B2", 40, 3, False)):
            gb = g[:, gi, :].unsqueeze(2).broadcast_to([128, KC, 2])
            if plus1:
                STT("dve", dst[:], mv[:, sc_lo:sc_lo + KC, :], 1.0, gb, ALU.add, ALU.mult,
                    [f"modv{pb}", f"gnorm{pb}"], [f"{nm}_{pb}"])
            else:
                TT("dve", dst[:], mv[:, sc_lo:sc_lo + KC, :], gb, ALU.mult,
                   [f"modv{pb}", f"gnorm{pb}"], [f"{nm}_{pb}"])

    def emit_rstd(sq, sqk, T):
        pr, prk = PS()
        for k in range(KC):
            MM(pr[:, 0:T], onesdiv[:], sq[:, k, 0:T], k == 0, k == KC - 1, [sqk, "c_onesdiv"], [prk])
        t1, t1k = tf.next()
        ACT(t1[:, 0:T], pr[:, 0:T], AF.Sqrt, [prk, "epsc"], [t1k], bias=epsc[:, 0:1], scale=1.0)
        rs, rsk = rst.next()
        P.add("dve", lambda e: e.reciprocal(rs[:, 0:T], t1[:, 0:T]), reads=[t1k], writes=[rsk])
        return rs, rsk

    def emit_modulate(dst, dstk, dcol, t0, T, s, Asc, Ak, sh_lo, pb, rs, rsk):
        mv = modv[pb]
        for k in range(KC):
            t1, t1k = tf.next()
            STT("dve", t1[:, 0:T], x_all[:, k, t0:t0 + T], Asc[:, k, s:s + 1], rs[:, 0:T], ALU.mult, ALU.mult,
                xkeys(t0, T) + [Ak, rsk], [t1k])
            ACT(dst[:, k, dcol:dcol + T], t1[:, 0:T], AF.Identity, [t1k, f"modv{pb}"], [dstk],
                bias=mv[:, sh_lo + k, s:s + 1], scale=1.0)

    def emit_layer(l):
        pb = l % 2
        P.set_epoch(l)
        wl = w_in[l].rearrange("(k p) n -> p k n", p=128)
        DMA("sp", cw[:], conv_w[l], [], ["cw"])
        DMA("sp", pscale[:], pool_scale[l], [], ["pscale"])
        for g in range(4):
            DMA("pool", pwbd[(g % 2) * 64:(g % 2) * 64 + 64, g // 2, (g % 2) * 64:(g % 2) * 64 + 64],
                pool_w[l, g], [], [("pwbd", g)])
        DMA("sp", rd[:], ret_decay[l].partition_broadcast(128), [], ["rd"])
        ACT(lg[:], rd[:], AF.Exp, ["rd"], ["lg"], scale=-1.0)
        ACT(lg[:], lg[:], AF.Ln, ["lg"], ["lg"], bias=1.0)
        TS("dve", lg[:], lg[:], -1.0, None, ALU.mult, None, ["lg"], ["lg"])
        for dr in range(2):
            for pr_ in range(2):
                for hh in range(2):
                    CP("dve", lgsel[hh * 64:hh * 64 + 64, dr * 2 + pr_:dr * 2 + pr_ + 1],
                       lg[hh * 64:hh * 64 + 64, dr * 4 + pr_ * 2 + hh:dr * 4 + pr_ * 2 + hh + 1], ["lg"], ["lgsel"])
        for h in range(4):
            t1, t1k = tf.next()
            TS("dve", t1[:, 0:128], cst["c_pm"][:], lg[:, h:h + 1], None, ALU.mult, None, ["lg", "c_pm"], [t1k])
            STT("dve", t1[:, 128:256], cst["c_nm"][:], lg[:, 4 + h:5 + h], t1[:, 0:128], ALU.mult, ALU.add,
                ["lg", "c_nm", t1k], [t1k])
            ACT(m_all[:, h, :], t1[:, 128:256], AF.Exp, [t1k], ["m_all"])
        ACT(kdt[:, 0:4], lg[:, 0:4], AF.Exp, ["lg", "c_jcols"], ["kdt"], scale=cst["c_jcols"][:, 0:1])
        ACT(kdt[:, 4:8], lg[:, 4:8], AF.Exp, ["lg", "c_jcols"], ["kdt"], scale=cst["c_jcols"][:, 1:2])
        for dr in range(2):
            src = cst["c_iota1"] if dr == 0 else cst["c_iotar"]
            for pr_ in range(2):
                ACT(qdt[:, dr, pr_, :], src[:], AF.Exp, ["lgsel", "c_iota1", "c_iotar"], ["qdt"],
                    scale=lgsel[:, dr * 2 + pr_:dr * 2 + pr_ + 1])
        ACT(cdt[:], lgsel[:], AF.Exp, ["lgsel"], ["cdt"], scale=128.0)

        with ExitStack() as st:
            sqr = Ring("n1sq", [128, KC, 512], BF16, 2, st)
            hr = Ring("n1h", [128, KC, 512], BF16, 2, st)
            for (t0, T, s) in TILES:
                sq, sqk = sqr.next()
                ACT(sq[:, :, 0:T], x_all[:, :, t0:t0 + T], AF.Square, xkeys(t0, T), [sqk])
                rs, rsk = emit_rstd(sq, sqk, T)
                hb, hk = hr.next()
                emit_modulate(hb, hk, 0, t0, T, s, A1[pb], f"A1_{pb}", 0, pb, rs, rsk)
                DMA("sp", h_scr[:, :, t0:t0 + T], hb[:, :, 0:T], [hk], [("hscr", t0)])
            P.barrier()
        chk("N1")

        def load_h(hr, t0, T):
            hb, hk = hr.next()
            DMA("sp", hb[:, :, 0:T], h_scr[:, :, t0:t0 + T], [("hscr", t0)], [hk])
            return hb, hk

        with ExitStack() as st:
            hr = Ring("rh", [128, KC, 512], BF16, 1, st)
            wq = wbig[:, 0:KC * 1024].rearrange("p (k n) -> p k n", k=KC)
            RW = ["wslot0", "wslot1"]
            rcos = sb("r_cos", [128, SEQ], BF16, st)
            rsin = sb("r_sin", [128, SEQ], BF16, st)
            DMA("sp", rcos[:], cos_d, [], ["r_cos"])
            DMA("sp", rsin[:], sin_d, [], ["r_sin"])
            qT = sb("qT", [128, 2, NTOK], BF16, st)
            kT = sb("kT", [128, 2, NTOK], BF16, st)
            sgT = sb("sgT", [128, 2, NTOK], BF16, st)
            vtok = sb("vtok", [128, NCH, 256], BF16, st)
            sball = sb("sball", [128, NCH, 2, 128], BF16, st)
            ktr = Ring("ktok", [128, 256], BF16, 2, st)
            vpr = Ring("vpr", [128, 256], BF16, 2, st)
            sst = [sb(f"sst{i}", [128, 2, 128], F32, st) for i in range(2)]
            sfb = Ring("sfb", [128, 2, 128], BF16, 2, st)
            scmr = Ring("scm", [128, 4, 128], BF16, 2, st)
            qpr = Ring("qpr", [128, 2, 2, 128], BF16, 2, st)
            qmr = Ring("qmr", [128, 2, 2, 128], BF16, 2, st)
            bmask3 = cst["c_bmask"][:].unsqueeze(1).broadcast_to([128, 2, 128])
            ysr = Ring("ysb", [128, 256], F32, 2, st)
            y2r = Ring("ysq", [128, 256], F32, 1, st)
            ynr = Ring("yn", [128, 256], BF16, 2, st)
            str_ = Ring("stat", [128, 16], F32, 2, st)
            brr = Ring("brr", [128, 2, 512], BF16, 2, st)
            DMA("pool", wq, wl[:, :, 0:1024], [], RW)
            for (t0, T, s) in TILES:
                hb, hk = load_h(hr, t0, T)
                for j in range(4):
                    isq = j < 2
                    dst = qT if isq else kT
                    dk = "qT" if isq else "kT"
                    pp, ppk = PS()
                    for k in range(KC):
                        MM(pp[:, 0:T], wq[:, k, j * 128:(j + 1) * 128], hb[:, k, 0:T], k == 0, k == KC - 1,
                           RW + [hk], [ppk])
                    if s == 1:
                        ACT(dst[:, j % 2, t0:t0 + T], pp[:, 0:T], AF.Copy, [ppk], [(dk, t0)],
                            scale=(0.125 if isq else 1.0))
                    else:
                        qb, qbk = tb.next()
                        ACT(qb[:, 0:T], pp[:, 0:T], AF.Copy, [ppk], [qbk], scale=(0.125 if isq else 1.0))
                        p2, p2k = PS()
                        MM(p2[:, 0:T], cst["c_ropeP"][:], qb[:, 0:T], True, True, [qbk, "c_ropeP"], [p2k])
                        t1, t1k = tf.next()
                        TT("dve", t1[:, 0:T], qb[:, 0:T], rcos[:, t0:t0 + T], ALU.mult, [qbk, "r_cos"], [t1k])
                        t2, t2k = tf.next()
                        TT("dve", t2[:, 0:T], p2[:, 0:T], rsin[:, t0:t0 + T], ALU.mult, [p2k, "r_sin"], [t2k])
                        TT("pool", dst[:, j % 2, t0:t0 + T], t1[:, 0:T], t2[:, 0:T], ALU.add, [t1k, t2k], [(dk, t0)])
                for jj in range(2):
                    pp, ppk = PS()
                    for k in range(KC):
                        MM(pp[:, 0:T], wq[:, k, 768 + jj * 128:768 + (jj + 1) * 128], hb[:, k, 0:T], k == 0,
                           k == KC - 1, RW + [hk], [ppk])
                    ACT(sgT[:, jj, t0:t0 + T], pp[:, 0:T], AF.Silu, [ppk], [("sgT", t0)])
                for sub in range(T // 128):
                    c = t0 // 128 + sub
                    pp, ppk = PS()
                    for k in range(KC):
                        MM(pp[:, 0:256], hb[:, k, sub * 128:(sub + 1) * 128], wq[:, k, 512:768], k == 0, k == KC - 1,
                           RW + [hk], [ppk])
                    CP("act", vtok[:, c, :], pp[:, 0:256], [ppk], [("vtok", c)])

            def tile_of(c):
                return (c * 128 // 512) * 512 if c < 16 else 2048

            def k_transpose(c):
                pt, ptk = PS()
                ptb = pt[:].bitcast(BF16)
                for pr_ in range(2):
                    TR(ptb[:, pr_ * 128:(pr_ + 1) * 128], kT[:, pr_, c * 128:(c + 1) * 128], [("kT", tile_of(c))], [ptk])
                kt, ktk = ktr.next()
                CP("act", kt[:], ptb[:, 0:256], [ptk], [ktk])
                return kt, ktk

            def delta_state(c, dr, kt, ktk):
                vp, vpk = vpr.next()
                TT("dve", vp[:].rearrange("p (h d) -> p h d", h=4), vtok[:, c, :].rearrange("p (h d) -> p h d", h=4),
                   kdt[:, dr * 4:dr * 4 + 4].unsqueeze(2).broadcast_to([128, 4, 64]), ALU.mult,
                   [("vtok", c), "kdt"], [vpk])
                pd, pdk = PS()
                for pr_ in range(2):
                    MM(pd[:, pr_ * 128:(pr_ + 1) * 128], kt[:, pr_ * 128:(pr_ + 1) * 128],
                       vp[:, pr_ * 128:(pr_ + 1) * 128], True, True, [ktk, vpk], [pdk])
                return pd, pdk

            def update_state(dr, pd, pdk):
                for pr_ in range(2):
                    STT("dve", sst[dr][:, pr_, :], sst[dr][:, pr_, :], cdt[:, dr * 2 + pr_:dr * 2 + pr_ + 1],
                        pd[:, pr_ * 128:(pr_ + 1) * 128], ALU.mult, ALU.add, [f"sst{dr}", "cdt", pdk], [f"sst{dr}"])

            MEMSET("dve", sst[1][:], 0.0, ["sst1"])
            MEMSET("dve", sst[0][:], 0.0, ["sst0"])
            for c in [17, 16] + list(range(15, -1, -1)):
                TT("pool", sball[:, c, :, :], sst[1][:], bmask3, ALU.mult, ["sst1", "c_bmask"], [("sball", c)])
                kt, ktk = k_transpose(c)
                pd, pdk = delta_state(c, 1, kt, ktk)
                update_state(1, pd, pdk)
            order = [16, 17] + list(range(16))
            br = None
            for c in order:
                tl = tile_of(c)
                sf, sfk = sfb.next()
                TT("pool", sf[:], sst[0][:], bmask3, ALU.mult, ["sst0", "c_bmask"], [sfk])
                qm, qmk = qmr.next()
                for pr_ in range(2):
                    TT("dve", qm[:, pr_, :, :], qT[:, pr_, c * 128:(c + 1) * 128].unsqueeze(1).broadcast_to([128, 2, 128]),
                       cst["c_hmask"][:], ALU.mult, [("qT", tl), "c_hmask"], [qmk])
                pscr, pscrk = PS()
                for pr_ in range(2):
                    MM(pscr[:, pr_ * 256:(pr_ + 1) * 256], kT[:, pr_, c * 128:(c + 1) * 128],
                       qm[:, pr_, :, :].rearrange("p a i -> p (a i)"), True, True, [("kT", tl), qmk], [pscrk])
                scm, scmk = scmr.next()
                TT("dve", scm[:], pscr[:].rearrange("p (h i) -> p h i", h=4), m_all[:], ALU.mult,
                   [pscrk, "m_all"], [scmk])
                qp, qpk = qpr.next()
                for dr in range(2):
                    TT("pool", qp[:, dr, :, :], qT[:, :, c * 128:(c + 1) * 128], qdt[:, dr, :, :], ALU.mult,
                       [("qT", tl), "qdt"], [qpk])
                py, pyk = PS()
                for pr_ in range(2):
                    MM(py[:, pr_ * 128:(pr_ + 1) * 128], qp[:, 0, pr_, :], sf[:, pr_, :], True, False, [qpk, sfk], [pyk])
                    MM(py[:, pr_ * 128:(pr_ + 1) * 128], qp[:, 1, pr_, :], sball[:, c, pr_, :], False, False,
                       [qpk, ("sball", c)], [pyk])
                    for hh in range(2):
                        h = pr_ * 2 + hh
                        MM(py[:, h * 64:(h + 1) * 64], scm[:, h, :], vtok[:, c, h * 64:(h + 1) * 64], False, hh == 1,
                           [scmk, ("vtok", c)], [pyk])
                kt, ktk = k_transpose(c)
                pd, pdk = delta_state(c, 0, kt, ktk)
                update_state(0, pd, pdk)
                ys, ysk = ysr.next()
                CP("act", ys[:], py[:, 0:256], [pyk], [ysk])
                y2, y2k = y2r.next()
                ACT(y2[:], py[:, 0:256], AF.Square, [pyk], [y2k])
                stt, stk = str_.next()
                P.add("dve", lambda e, stt=stt, ys=ys: e.reduce_sum(stt[:, 0:4], ys[:].rearrange("p (h d) -> p h d", h=4),
                                                                    axis=AX.X), reads=[ysk], writes=[stk])
                P.add("dve", lambda e, stt=stt, y2=y2: e.reduce_sum(stt[:, 4:8], y2[:].rearrange("p (h d) -> p h d", h=4),
                                                                    axis=AX.X), reads=[y2k, stk], writes=[stk])
                TS("dve", stt[:, 0:4], stt[:, 0:4], 1.0 / 64, None, ALU.mult, None, [stk], [stk])
                TT("dve", stt[:, 8:12], stt[:, 0:4], stt[:, 0:4], ALU.mult, [stk], [stk])
                STT("dve", stt[:, 4:8], stt[:, 4:8], 1.0 / 64, stt[:, 8:12], ALU.mult, ALU.subtract, [stk], [stk])
                ACT(stt[:, 8:12], stt[:, 4:8], AF.Sqrt, [stk, "epsc"], [stk], bias=epsc[:, 0:1], scale=1.0)
                P.add("dve", lambda e, stt=stt: e.reciprocal(stt[:, 12:16], stt[:, 8:12]), reads=[stk], writes=[stk])
                TT("dve", ys[:].rearrange("p (h d) -> p h d", h=4), ys[:].rearrange("p (h d) -> p h d", h=4),
                   stt[:, 0:4].unsqueeze(2).broadcast_to([128, 4, 64]), ALU.subtract, [ysk, stk], [ysk])
                yn, ynk = ynr.next()
                TT("dve", yn[:].rearrange("p (h d) -> p h d", h=4), ys[:].rearrange("p (h d) -> p h d", h=4),
                   stt[:, 12:16].unsqueeze(2).broadcast_to([128, 4, 64]), ALU.mult, [ysk, stk], [ynk])
                pt, ptk = PS()
                ptb = pt[:].bitcast(BF16)
                for pr_ in range(2):
                    TR(ptb[:, pr_ * 128:(pr_ + 1) * 128], yn[:, pr_ * 128:(pr_ + 1) * 128], [ynk], [ptk])
                sub = (c * 128 - tl) // 128
                if sub == 0:
                    br, brk = brr.next()
                TT("dve", br[:, :, sub * 128:(sub + 1) * 128], ptb[:, 0:256].rearrange("p (a t) -> p a t", a=2),
                   sgT[:, :, c * 128:(c + 1) * 128], ALU.mult, [ptk, ("sgT", tl)], [brk])
                last = (sub == 3) if c < 16 else (sub == 1)
                if last:
                    T = 512 if c < 16 else 256
                    DMA("sp", br_scr[:, 0:2, tl:tl + T], br[:, :, 0:T], [brk], [("brscr", 0, tl)])
            P.barrier()
        chk("R")

        with ExitStack() as st:
            hr = Ring("ch", [128, KC, 512], BF16, 1, st)
            wc = wbig[:, 0:KC * 768].rearrange("p (k n) -> p k n", k=KC)
            CW = ["wslot0", "wslot1"]
            convB = sb("convB", [128, 2, NTOK], BF16, st)
            convu = sb("convu", [128, 2, NTOK + 4], BF16, st)
            brr = Ring("cbr", [128, 2, 512], BF16, 2, st)
            DMA("pool", wc, wl[:, :, 1024:1792], [], CW)
            MEMSET("pool", convu[:], 0.0, ["convu"])

            def cpos(t0):
                return t0 + 1 if t0 < SEQ else t0 + 3
            for (t0, T, s) in TILES:
                hb, hk = load_h(hr, t0, T)
                for jj in range(2):
                    pp, ppk = PS()
                    for k in range(KC):
                        MM(pp[:, 0:T], wc[:, k, jj * 128:(jj + 1) * 128], hb[:, k, 0:T], k == 0, k == KC - 1,
                           CW + [hk], [ppk])
                    CP("act", convB[:, jj, t0:t0 + T], pp[:, 0:T], [ppk], [("convB", t0)])
                    pc, pck = PS()
                    for k in range(KC):
                        MM(pc[:, 0:T], wc[:, k, 256 + jj * 128:256 + (jj + 1) * 128], hb[:, k, 0:T], k == 0, k == KC - 1,
                           CW + [hk], [pck])
                    px, pxk = PS()
                    for k in range(KC):
                        MM(px[:, 0:T], wc[:, k, 512 + jj * 128:512 + (jj + 1) * 128], hb[:, k, 0:T], k == 0, k == KC - 1,
                           CW + [hk], [pxk])
                    t1, t1k = tf.next()
                    CP("act", t1[:, 0:T], pc[:, 0:T], [pck], [t1k])
                    p0 = cpos(t0)
                    TT("dve", convu[:, jj, p0:p0 + T], px[:, 0:T], t1[:, 0:T], ALU.mult, [pxk, t1k], ["convu"])
            for (t0, T, s) in TILES:
                br, brk = brr.next()
                p0 = cpos(t0)
                for jj in range(2):
                    t1, t1k = tf.next()
                    TS("dve", t1[:, 0:T], convu[:, jj, p0 - 1:p0 - 1 + T], cw[:, jj, 0:1], None, ALU.mult, None,
                       ["convu", "cw"], [t1k])
                    t2, t2k = tf.next()
                    STT("dve", t2[:, 0:T], convu[:, jj, p0:p0 + T], cw[:, jj, 1:2], t1[:, 0:T], ALU.mult, ALU.add,
                        ["convu", "cw", t1k], [t2k])
                    t3, t3k = tf.next()
                    STT("dve", t3[:, 0:T], convu[:, jj, p0 + 1:p0 + 1 + T], cw[:, jj, 2:3], t2[:, 0:T], ALU.mult, ALU.add,
                        ["convu", "cw", t2k], [t3k])
                    TT("pool", br[:, jj, 0:T], t3[:, 0:T], convB[:, jj, t0:t0 + T], ALU.mult, [t3k, ("convB", t0)], [brk])
                DMA("sp", br_scr[:, 2:4, t0:t0 + T], br[:, :, 0:T], [brk], [("brscr", 1, t0)])
            P.barrier()
        chk("C")

        with ExitStack() as st:
            hr = Ring("fh", [128, KC, 512], BF16, 1, st)
            wf = wbig[:, 0:KC * 256].rearrange("p (k n) -> p k n", k=KC)
            ucs = sb("ucs", [128, NCH, 2, 256], BF16, st)
            utr = Ring("fu", [128, 2, 512], BF16, 2, st)
            tabc = Ring("ftc", [128, 16, 512], BF16, 1, st)
            tabs = Ring("fts", [128, 16, 512], BF16, 1, st)
            brr = Ring("fbr", [128, 2, 512], BF16, 2, st)
            DMA("pool", wf, wl[:, :, 1792:2048], [], ["wslot0"])
            for (t0, T, s) in TILES:
                hb, hk = load_h(hr, t0, T)
                ut, utk = utr.next()
                for ch in range(2):
                    pp, ppk = PS()
                    for k in range(KC):
                        MM(pp[:, 0:T], wf[:, k, ch * 128:(ch + 1) * 128], hb[:, k, 0:T], k == 0, k == KC - 1,
                           ["wslot0", hk], [ppk])
                    CP("act", ut[:, ch, 0:T], pp[:, 0:T], [ppk], [utk])
                for sub in range(T // 128):
                    c = t0 // 128 + sub
                    pp, ppk = PS()
                    for ch in range(2):
                        MM(pp[:, ch * 256:(ch + 1) * 256], ut[:, ch, sub * 128:(sub + 1) * 128], cst["c_cs64"][:],
                           True, True, [utk, "c_cs64"], [ppk])
                    CP("dve", ucs[:, c, :, :], pp[:].rearrange("p (a n) -> p a n", a=2), [ppk], [("ucs", c)])
            dcv = dftc.rearrange("(n p) k -> p n k", p=128)
            dsv = dfts.rearrange("(n p) k -> p n k", p=128)
            sc_lat = 1.0 / float(np.sqrt(SEQ * 64.0))
            for kt_ in range(4):
                tc_, tck = tabc.next()
                ts_, tsk = tabs.next()
                DMA("sp", tc_[:], dcv[:, :, kt_ * 512:(kt_ + 1) * 512], [], [tck])
                DMA("sp", ts_[:], dsv[:, :, kt_ * 512:(kt_ + 1) * 512], [], [tsk])
                br, brk = brr.next()
                for ch in range(2):
                    pp, ppk = PS()
                    for n in range(16):
                        MM(pp[:], ucs[:, n, ch, 0:128], tc_[:, n, :], n == 0, False, [("ucs", n), tck], [ppk])
                        MM(pp[:], ucs[:, n, ch, 128:256], ts_[:, n, :], False, n == 15, [("ucs", n), tsk], [ppk])
                    ACT(br[:, ch, :], pp[:], AF.Copy, [ppk], [brk], scale=sc_lat)
                DMA("sp", br_scr[:, 4:6, kt_ * 512:(kt_ + 1) * 512], br[:], [brk], [("brscr", 2, kt_ * 512)])
            br, brk = brr.next()
            dx = cst["c_dftctx"]
            for ch in range(2):
                pp, ppk = PS()
                for n in range(2):
                    MM(pp[:, 0:CTX], ucs[:, 16 + n, ch, 0:128], dx[:, 0, n, :], n == 0, False,
                       [("ucs", 16 + n), "c_dftctx"], [ppk])
                    MM(pp[:, 0:CTX], ucs[:, 16 + n, ch, 128:256], dx[:, 1, n, :], False, n == 1,
                       [("ucs", 16 + n), "c_dftctx"], [ppk])
                ACT(br[:, ch, 0:CTX], pp[:, 0:CTX], AF.Copy, [ppk], [brk], scale=1.0 / 128.0)
            DMA("sp", br_scr[:, 4:6, SEQ:NTOK], br[:, :, 0:CTX], [brk], [("brscr", 2, SEQ)])
            P.barrier()
        chk("F")

        with ExitStack() as st:
            hr = Ring("ph", [128, KC, 512], BF16, 1, st)
            wp = wbig[:, 0:KC * 256].rearrange("p (k n) -> p k n", k=KC)
            PW = NTOK + 64
            pu = sb("poolu", [128, 2, PW], BF16, st)
            a2 = sb("pa2", [128, 2, 528], F32, st)
            a4 = sb("pa4", [128, 2, 528], F32, st)
            a8 = sb("pa8", [128, 2, 528], F32, st)
            a16 = sb("pa16", [128, 2, 528], F32, st)
            pld = Ring("pld", [128, 2, 512], BF16, 2, st)
            brr = Ring("pbr", [128, 2, 512], BF16, 2, st)
            ped = Ring("ped", [128, 16], F32, 2, st)
            DMA("pool", wp, wl[:, :, 2048:2304], [], ["wslot0"])
            MEMSET("pool", pu[:], 0.0, ["poolu"])

            def ppos(t0):
                return t0 + 16 if t0 < SEQ else t0 + 48
            for (t0, T, s) in TILES:
                hb, hk = load_h(hr, t0, T)
                for ch in range(2):
                    pp, ppk = PS()
                    for k in range(KC):
                        MM(pp[:, 0:T], wp[:, k, ch * 128:(ch + 1) * 128], hb[:, k, 0:T], k == 0, k == KC - 1,
                           ["wslot0", hk], [ppk])
                    p0 = ppos(t0)
                    CP("act", pu[:, ch, p0:p0 + T], pp[:, 0:T], [ppk], ["poolu"])
            for (t0, T, s) in TILES:
                p0 = ppos(t0)
                L = T + 16
                b0 = p0 - 8
                TT("dve", a2[:, :, 1:L], pu[:, :, b0 + 1:b0 + L], pu[:, :, b0:b0 + L - 1], ALU.add, ["poolu"], ["pa2"])
                TT("pool", a4[:, :, 3:L], a2[:, :, 3:L], a2[:, :, 1:L - 2], ALU.add, ["pa2"], ["pa4"])
                TT("dve", a8[:, :, 7:L], a4[:, :, 7:L], a4[:, :, 3:L - 4], ALU.add, ["pa4"], ["pa8"])
                TT("pool", a16[:, :, 15:L], a8[:, :, 15:L], a8[:, :, 7:L - 8], ALU.add, ["pa8"], ["pa16"])
                pl, plk = pld.next()
                wins = [(a2, "pa2", 0, 2), (a4, "pa4", 1, 4), (a8, "pa8", 3, 8), (a16, "pa16", 7, 16)]
                for g in range(4):
                    ch, hh = g // 2, g % 2
                    aw, awk, sh, w = wins[g]
                    lo = hh * 64
                    STT("dve", pl[lo:lo + 64, ch, 0:T], aw[lo:lo + 64, ch, 8 + sh:8 + sh + T], 1.0 / w,
                        pu[lo:lo + 64, ch, p0:p0 + T], ALU.mult, ALU.subtract, [awk, "poolu"], [plk])
                    edges = []
                    if t0 == 0 or t0 == SEQ:
                        edges.append((0, 0))
                    if t0 + T == SEQ or t0 + T == NTOK:
                        edges.append((T - 8, 8))
                    for (e0, r0) in edges:
                        pe_, pek = ped.next()
                        TT("dve", pe_[lo:lo + 64, 0:8], aw[lo:lo + 64, ch, 8 + sh + e0:8 + sh + e0 + 8],
                           cst["c_pedge"][lo:lo + 64, ch, r0:r0 + 8], ALU.mult, [awk, "c_pedge"], [pek])
                        TT("dve", pl[lo:lo + 64, ch, e0:e0 + 8], pe_[lo:lo + 64, 0:8],
                           pu[lo:lo + 64, ch, p0 + e0:p0 + e0 + 8], ALU.subtract, [pek, "poolu", plk], [plk])
                br, brk = brr.next()
                for ch in range(2):
                    pp, ppk = PS()
                    MM(pp[:, 0:T], pwbd[:, ch, :], pl[:, ch, 0:T], True, True, [("pwbd", 2 * ch), ("pwbd", 2 * ch + 1), plk], [ppk])
                    ACT(br[:, ch, 0:T], pp[:, 0:T], AF.Copy, [ppk, "pscale"], [brk], scale=pscale[:, ch:ch + 1])
                DMA("sp", br_scr[:, 6:8, t0:t0 + T], br[:, :, 0:T], [brk], [("brscr", 3, t0)])
            P.barrier()
        chk("P")

        if l + 1 < depth:
            emit_mod(l + 1)

        last_layer = (l == DEPTH - 1)
        for st_tiles in SUPER:
            if last_layer:
                st_tiles = [t for t in st_tiles if t[2] == 0]
            base = st_tiles[0][0]
            offs = []
            o = 0
            for (t0, T, s) in st_tiles:
                offs.append(o)
                o += T
            W = o
            with ExitStack() as sm:
                merged = sb("merged", [128, KC, 768], BF16, sm)
                with ExitStack() as st:
                    hT = sb("g_h", [128, KC, 768], BF16, st)
                    bT = sb("g_b", [128, 8, 768], BF16, st)
                    acc = sb("g_acc", [128, KC, 768], F32, st)
                    wbr = Ring("g_wb", [128, 2, 512], BF16, 2, st)
                    for ti, (t0, T, s) in enumerate(st_tiles):
                        DMA("sp", hT[:, :, offs[ti]:offs[ti] + T], h_scr[:, :, t0:t0 + T], [("hscr", (t0 // 512) * 512 if t0 < SEQ else SEQ)],
                            [("g_h", ti)])
                        DMA("sp", bT[:, :, offs[ti]:offs[ti] + T], br_scr[:, :, t0:t0 + T],
                            [("brscr", b_, tt_) for b_ in range(4) for tt_ in ([(t0 // 512) * 512] if t0 < SEQ else [SEQ])],
                            [("g_b", ti)])
                    for i in range(4):
                        for half in range(2):
                            ws, wk = next_slot()
                            wg = ws[:, 0:KC * 512].rearrange("p (k n) -> p k n", k=KC)
                            c0 = MIXW + i * D + half * 512
                            DMA("pool", wg, wl[:, :, c0:c0 + 512], [], [wk])
                            wb, wbk = wbr.next()
                            DMA("pool", wb[:], w_branch[l, i].rearrange("(k p) n -> p k n", p=128)[:, :, half * 512:(half + 1) * 512],
                                [], [wbk])
                            for c4 in range(4):
                                c = half * 4 + c4
                                for ti, (t0, T, s) in enumerate(st_tiles):
                                    o_ = offs[ti]
                                    pg, pgk = PS()
                                    for k in range(KC):
                                        MM(pg[:, 0:T], wg[:, k, c4 * 128:(c4 + 1) * 128], hT[:, k, o_:o_ + T], k == 0,
                                           k == KC - 1, [wk, ("g_h", ti)], [pgk])
                                    pbp, pbk = PS()
                                    for kc in range(2):
                                        MM(pbp[:, 0:T], wb[:, kc, c4 * 128:(c4 + 1) * 128], bT[:, i * 2 + kc, o_:o_ + T],
                                           kc == 0, kc == 1, [wbk, ("g_b", ti)], [pbk])
                                    sg, sgk = tb.next()
                                    ACT(sg[:, 0:T], pg[:, 0:T], AF.Sigmoid, [pgk], [sgk])
                                    ak = ("g_acc", c, ti)
                                    if i == 0:
                                        TT("dve", acc[:, c, o_:o_ + T], pbp[:, 0:T], sg[:, 0:T], ALU.mult, [pbk, sgk], [ak])
                                    else:
                                        t1, t1k = tf.next()
                                        TT("dve", t1[:, 0:T], pbp[:, 0:T], sg[:, 0:T], ALU.mult, [pbk, sgk], [t1k])
                                        if i < 3:
                                            TT("pool", acc[:, c, o_:o_ + T], acc[:, c, o_:o_ + T], t1[:, 0:T], ALU.add,
                                               [ak, t1k], [ak])
                                        else:
                                            TT("pool", merged[:, c, o_:o_ + T], acc[:, c, o_:o_ + T], t1[:, 0:T], ALU.add,
                                               [ak, t1k], [("merged", ti)])
                    P.barrier()
                with ExitStack() as st:
                    mix = sb("o_mix", [128, KC, 512], F32, st)
                    msq = sb("o_sq", [128, KC, 512], BF16, st)
                    wos = []
                    for half in range(2):
                        ws, wk = next_slot()
                        wo = ws[:, 0:KC * 512].rearrange("p (k n) -> p k n", k=KC)
                        DMA("pool", wo, w_o[l].rearrange("(k p) n -> p k n", p=128)[:, :, half * 512:(half + 1) * 512], [], [wk])
                        wos.append((wo, wk))
                    for ti, (t0, T, s) in enumerate(st_tiles):
                        o_ = offs[ti]
                        for c in range(KC):
                            wo, wk = wos[c // 4]
                            pp, ppk = PS()
                            for k in range(KC):
                                MM(pp[:, 0:T], wo[:, k, (c % 4) * 128:(c % 4 + 1) * 128], merged[:, k, o_:o_ + T], k == 0,
                                   k == KC - 1, [wk, ("merged", ti)], [ppk])
                            CP("act", mix[:, c, 0:T], pp[:, 0:T], [ppk], ["o_mix"])
                            ACT(msq[:, c, 0:T], pp[:, 0:T], AF.Square, [ppk], ["o_sq"])
                        rs, rsk = emit_rstd(msq, "o_sq", T)
                        for c in range(KC):
                            t1, t1k = tf.next()
                            STT("dve", t1[:, 0:T], mix[:, c, 0:T], B1[pb][:, c, s:s + 1], rs[:, 0:T], ALU.mult, ALU.mult,
                                ["o_mix", f"B1_{pb}", rsk], [t1k])
                            TT("pool", x_all[:, c, t0:t0 + T], x_all[:, c, t0:t0 + T], t1[:, 0:T], ALU.add,
                               xkeys(t0, T) + [t1k], xkeys(t0, T))
                    P.barrier()
            with ExitStack() as st:
                h2 = sb("f_h2", [128, KC, 768], BF16, st)
                hid = sb("f_hid", [128, 11, 768], BF16, st)
                fo = sb("f_out", [128, KC, 768], F32, st)
                fsq = sb("f_sq", [128, KC, 512], BF16, st)
                for ti, (t0, T, s) in enumerate(st_tiles):
                    ACT(fsq[:, :, 0:T], x_all[:, :, t0:t0 + T], AF.Square, xkeys(t0, T), ["f_sq"])
                    rs, rsk = emit_rstd(fsq, "f_sq", T)
                    emit_modulate(h2, ("f_h2", ti), offs[ti], t0, T, s, A2[pb], f"A2_{pb}", 24, pb, rs, rsk)
                wu = w_up[l].rearrange("(k p) n -> p k n", p=128)
                for half in range(2):
                    for blk in range(3):
                        ncb = 4 if blk < 2 else 3
                        a0 = half * 1408 + blk * 512
                        wsa, wak = next_slot()
                        wa = wsa[:, 0:KC * 512].rearrange("p (k n) -> p k n", k=KC)
                        DMA("pool", wa[:, :, 0:ncb * 128], wu[:, :, a0:a0 + ncb * 128], [], [wak])
                        wsu, wuk = next_slot()
                        wuu = wsu[:, 0:KC * 512].rearrange("p (k n) -> p k n", k=KC)
                        DMA("pool", wuu[:, :, 0:ncb * 128], wu[:, :, DFF + a0:DFF + a0 + ncb * 128], [], [wuk])
                        for cb in range(ncb):
                            jj = blk * 4 + cb
                            for ti, (t0, T, s) in enumerate(st_tiles):
                                o_ = offs[ti]
                                pa, pak = PS()
                                for k in range(KC):
                                    MM(pa[:, 0:T], wa[:, k, cb * 128:(cb + 1) * 128], h2[:, k, o_:o_ + T], k == 0, k == KC - 1,
                                       [wak, ("f_h2", ti)], [pak])
                                pu_, puk = PS()
                                for k in range(KC):
                                    MM(pu_[:, 0:T], wuu[:, k, cb * 128:(cb + 1) * 128], h2[:, k, o_:o_ + T], k == 0, k == KC - 1,
                                       [wuk, ("f_h2", ti)], [puk])
                                sa, sak = tb.next()
                                ACT(sa[:, 0:T], pa[:, 0:T], AF.Silu, [pak], [sak])
                                TT("dve", hid[:, jj, o_:o_ + T], pu_[:, 0:T], sa[:, 0:T], ALU.mult, [puk, sak], [("f_hid", ti)])
                    wdv = w_down[l, half * 1408:(half + 1) * 1408, :].rearrange("(j p) n -> p j n", p=128)
                    for cb2 in range(4):
                        ws, wk = next_slot()
                        wd = ws[:, 0:11 * 256].rearrange("p (j n) -> p j n", j=11)
                        DMA("pool", wd, wdv[:, :, cb2 * 256:(cb2 + 1) * 256], [], [wk])
                        for c2 in range(2):
                            c = cb2 * 2 + c2
                            for ti, (t0, T, s) in enumerate(st_tiles):
                                o_ = offs[ti]
                                pp, ppk = PS()
                                for jj in range(11):
                                    MM(pp[:, 0:T], wd[:, jj, c2 * 128:(c2 + 1) * 128], hid[:, jj, o_:o_ + T], jj == 0, jj == 10,
                                       [wk, ("f_hid", ti)], [ppk])
                                fk = ("f_out", c, ti)
                                if half == 0:
                                    CP("act", fo[:, c, o_:o_ + T], pp[:, 0:T], [ppk], [fk])
                                else:
                                    TT("dve", fo[:, c, o_:o_ + T], fo[:, c, o_:o_ + T], pp[:, 0:T], ALU.add, [fk, ppk], [fk])
                for ti, (t0, T, s) in enumerate(st_tiles):
                    o_ = offs[ti]
                    fks = [("f_out", c, ti) for c in range(KC)]
                    ACT(fsq[:, :, 0:T], fo[:, :, o_:o_ + T], AF.Square, fks, ["f_sq"])
                    rs, rsk = emit_rstd(fsq, "f_sq", T)
                    for c in range(KC):
                        t1, t1k = tf.next()
                        STT("dve", t1[:, 0:T], fo[:, c, o_:o_ + T], B2[pb][:, c, s:s + 1], rs[:, 0:T], ALU.mult, ALU.mult,
                            [("f_out", c, ti), f"B2_{pb}", rsk], [t1k])
                        TT("pool", x_all[:, c, t0:t0 + T], x_all[:, c, t0:t0 + T], t1[:, 0:T], ALU.add,
                           xkeys(t0, T) + [t1k], xkeys(t0, T))
                P.barrier()

    emit_mod(0)
    try:
        chk("M")
        for l in range(depth):
            emit_layer(l)
    except _Stop:
        pass
    for (t0, T, s) in TILES:
        if s == 0:
            DMA("sp", yout[:, :, t0:t0 + T], x_all[:, :, t0:t0 + T], xkeys(t0, T), [("yout", t0)])
    if dbg is not None:
        dbg[0](nc, P, DMA, dbg_out, locals())
    P.finish("sp")
    top.close()
    return nc, P


_CONSTS = None


def _prep_inputs(x, c, ctx, c_ctx, w_mod, b_mod, norm_g, w_in, ret_decay, conv_w, pool_w, pool_scale,
                 w_branch, w_o, ffn_w_up, ffn_w_down):
    global _CONSTS
    if _CONSTS is None:
        _CONSTS = make_consts()
    f = lambda a: np.ascontiguousarray(np.asarray(a, dtype=np.float32))
    x, c, ctx, c_ctx = f(x), f(c), f(ctx), f(c_ctx)
    shared = {
        "w_mod": f(w_mod),
        "b_mod": np.ascontiguousarray(f(b_mod).reshape(DEPTH, 48, 128).transpose(0, 2, 1)),
        "norm_g": np.ascontiguousarray(f(norm_g).reshape(DEPTH, 4, KC, 128).transpose(0, 3, 1, 2)),
        "w_in": f(w_in),
        "ret_decay": f(ret_decay).reshape(DEPTH, 8),
        "conv_w": np.ascontiguousarray(f(conv_w).reshape(DEPTH, 3, 2, 128).transpose(0, 3, 2, 1)),
        "pool_w": f(pool_w),
        "pool_scale": np.ascontiguousarray(f(pool_scale).reshape(DEPTH, 2, 128).transpose(0, 2, 1)),
        "w_branch": f(w_branch),
        "w_o": f(w_o),
        "ffn_w_up": f(ffn_w_up),
        "ffn_w_down": f(ffn_w_down),
    }
    shared.update(_CONSTS)
    in_maps = []
    for b in range(x.shape[0]):
        xa = np.concatenate([x[b].T, ctx[b].T], axis=1)
        xa = np.ascontiguousarray(xa.reshape(KC, 128, NTOK).transpose(1, 0, 2))
        cc = np.stack([c[b], c_ctx], axis=1)
        cc = np.ascontiguousarray(cc.reshape(KC, 128, 2).transpose(1, 0, 2))
        m = dict(shared)
        m["xin"] = xa
        m["c2"] = cc
        in_maps.append(m)
    return in_maps


_NC = {}


def kernel(**inputs):
    in_maps = _prep_inputs(**inputs)
    if "nc" not in _NC:
        _NC["nc"] = build(DEPTH)[0]
    nc = _NC["nc"]
    res = run_bass_kernel_spmd(nc, in_maps, core_ids=list(range(len(in_maps))))
    outs = []
    for r in res.results:
        y = np.asarray(r["yout"], dtype=np.float32)
        outs.append(y.transpose(1, 0, 2).reshape(D, SEQ).T)
    return np.ascontiguousarray(np.stack(outs, axis=0))
```

```python
import numpy as np
import ml_dtypes
import concourse.bass as bass
import concourse.mybir as mybir
from concourse.bass_utils import run_bass_kernel_spmd

F32 = mybir.dt.float32
BF16 = mybir.dt.bfloat16
ALU = mybir.AluOpType
AF = mybir.ActivationFunctionType
AX = mybir.AxisListType

D = 1024
KC = 8
SEQ = 2048
CTX = 256
NTOK = SEQ + CTX
NCH = NTOK // 128
DEPTH = 4
MIXW = 2304
DFF = 2816
EPS = 1e-6
TILES = [(0, 512, 0), (512, 512, 0), (1024, 512, 0), (1536, 512, 0), (2048, 256, 1)]
SUPER = [[(0, 512, 0), (512, 256, 0)], [(768, 512, 0), (1280, 256, 0)], [(1536, 512, 0), (2048, 256, 1)]]


class _Op:
    __slots__ = ("eng", "sem", "count", "dma", "nosig")


class Prog:
    NPOOL = 12

    def __init__(self, nc, n_epochs=1):
        self.nc = nc
        self.engs = {"pe": nc.tensor, "act": nc.scalar, "dve": nc.vector,
                     "pool": nc.gpsimd, "sp": nc.sync}
        self.esem = {e: [nc.alloc_semaphore(name=f"s_{e}_{i}") for i in range(n_epochs)]
                     for e in self.engs}
        self.ecnt = {e: [0] * n_epochs for e in self.engs}
        self.dsem = {e: [nc.alloc_semaphore(name=f"d_{e}_{i}") for i in range(self.NPOOL)]
                     for e in ("sp", "pool")}
        self.dcnt = {e: [0] * self.NPOOL for e in self.dsem}
        self.dlast = {e: [None] * self.NPOOL for e in self.dsem}
        self.dnext = {e: 0 for e in self.dsem}
        self.epoch = 0
        self.last_writer = {}
        self.readers = {}
        self.known = {e: {} for e in self.engs}
        self.last_op = {e: None for e in self.engs}
        self.pending_pe = []
        self.nops = 0
        self.nwaits = 0

    def set_epoch(self, i):
        self.epoch = i

    def _wait(self, eng, sem, val):
        kn = self.known[eng]
        key = id(sem)
        if kn.get(key, 0) >= val:
            return
        kn[key] = val
        self.engs[eng].wait_ge(sem, val)
        self.nwaits += 1

    def add(self, eng, emit, reads=(), writes=(), dma=False, nosig=False):
        deps = []
        for k in reads:
            w = self.last_writer.get(k)
            if w is not None:
                deps.append(w)
        for k in writes:
            w = self.last_writer.get(k)
            if w is not None:
                deps.append(w)
            deps.extend(self.readers.get(k, ()))
        for d in deps:
            if (not d.dma) and (not dma) and d.eng == "pe" and eng == "pe":
                continue
            if d.sem is None:
                raise RuntimeError("dependency on uncovered nosig PE op")
            self._wait(eng, d.sem, d.count)
        op = _Op()
        op.eng = eng
        op.dma = dma
        op.nosig = nosig
        op.sem = None
        op.count = 0
        if dma:
            i = self.dnext[eng]
            self.dnext[eng] = (i + 1) % self.NPOOL
            sem = self.dsem[eng][i]
            if self.dcnt[eng][i] > 0:
                self._wait(eng, sem, self.dcnt[eng][i])
            self.dcnt[eng][i] += 16
            op.sem = sem
            op.count = self.dcnt[eng][i]
            self.dlast[eng][i] = op
            ins = emit(self.engs[eng])
            ins.then_inc(sem, 16)
        else:
            ins = emit(self.engs[eng])
            if nosig:
                self.pending_pe.append(op)
            else:
                self.ecnt[eng][self.epoch] += 1
                op.sem = self.esem[eng][self.epoch]
                op.count = self.ecnt[eng][self.epoch]
                ins.then_inc(op.sem, 1)
                if eng == "pe" and self.pending_pe:
                    for p in self.pending_pe:
                        p.sem = op.sem
                        p.count = op.count
                    self.pending_pe = []
            self.last_op[eng] = op
        for k in reads:
            self.readers.setdefault(k, []).append(op)
        for k in writes:
            self.last_writer[k] = op
            self.readers[k] = []
        self.nops += 1
        return op

    def barrier(self):
        assert not self.pending_pe
        ops = [o for o in self.last_op.values() if o is not None]
        for e in self.dlast:
            ops.extend(o for o in self.dlast[e] if o is not None)
        for eng in self.engs:
            for d in ops:
                self._wait(eng, d.sem, d.count)

    def finish(self, eng="sp"):
        for e in self.dlast:
            for o in self.dlast[e]:
                if o is not None:
                    self._wait(eng, o.sem, o.count)


def make_consts():
    bf = ml_dtypes.bfloat16
    c = {}
    c["c_ident"] = np.eye(128, dtype=np.float32).astype(bf)
    c["c_onesdiv"] = np.full((128, 128), 1.0 / 1024, np.float32).astype(bf)
    e = np.arange(128) % 64
    fi = (e % 16).astype(np.float64)
    freq = 10000.0 ** (-fi / 16.0)
    t = np.arange(SEQ)
    rows = (t // 64).astype(np.float64)
    cols = (t % 64).astype(np.float64)
    pos = np.where((e < 32)[:, None], rows[None, :], cols[None, :])
    ang = pos * freq[:, None]
    x1 = (e % 32) < 16
    c["c_cos"] = np.cos(ang).astype(np.float32).astype(bf)
    c["c_sin"] = np.where(x1[:, None], -np.sin(ang), np.sin(ang)).astype(np.float32).astype(bf)
    partner = np.where(x1, np.arange(128) + 16, np.arange(128) - 16)
    rp = np.zeros((128, 128), np.float32)
    rp[partner, np.arange(128)] = 1.0
    c["c_ropeP"] = rp.astype(bf)
    j = np.arange(128)[:, None].astype(np.float32)
    i = np.arange(128)[None, :].astype(np.float32)
    c["c_pm"] = np.maximum(i - j, 0).astype(np.float32)
    c["c_nm"] = np.maximum(j - i, 0).astype(np.float32)
    c["c_iota1"] = np.broadcast_to(i + 1.0, (128, 128)).astype(np.float32).copy()
    c["c_iotar"] = np.broadcast_to(128.0 - i, (128, 128)).astype(np.float32).copy()
    pidx = np.arange(128)
    hm = np.zeros((128, 2, 128), np.float32)
    for hh in range(2):
        hm[pidx // 64 == hh, hh, :] = 1.0
    c["c_hmask"] = hm.astype(bf)
    c["c_bmask"] = (pidx[:, None] // 64 == pidx[None, :] // 64).astype(np.float32)
    c["c_jcols"] = np.stack([127.0 - np.arange(128), np.arange(128) * 1.0], axis=1).astype(np.float32)
    cs = np.zeros((128, 256), np.float64)
    jj = np.arange(64)[:, None]
    mm = np.arange(64)[None, :]
    a = 2 * np.pi * ((jj * mm) % 64) / 64.0
    for g in range(2):
        cs[g * 64:(g + 1) * 64, g * 64:(g + 1) * 64] = np.cos(a)
        cs[g * 64:(g + 1) * 64, 128 + g * 64:128 + (g + 1) * 64] = np.sin(a)
    c["c_cs64"] = cs.astype(np.float32).astype(bf)
    n = np.arange(SEQ)
    a = 2 * np.pi * ((n[:, None] * n[None, :]) % SEQ) / float(SEQ)
    c["c_dftc"] = np.cos(a).astype(np.float32).astype(bf)
    c["c_dfts"] = (-np.sin(a)).astype(np.float32).astype(bf)
    n = np.arange(CTX)
    a = 2 * np.pi * ((n[:, None] * n[None, :]) % CTX) / float(CTX)
    cc = np.cos(a).astype(np.float32).reshape(2, 128, CTX).transpose(1, 0, 2)
    sc = (-np.sin(a)).astype(np.float32).reshape(2, 128, CTX).transpose(1, 0, 2)
    c["c_dftctx"] = np.ascontiguousarray(np.stack([cc, sc], axis=1)).astype(bf)
    rc = np.zeros((4, 16), np.float32)
    nn = SEQ
    tt = np.arange(nn)
    for g, w in enumerate((2, 4, 8, 16)):
        lo = np.clip(tt - w // 2, 0, nn)
        hi = np.clip(tt - w // 2 + w, 0, nn)
        r = 1.0 / (hi - lo).astype(np.float32)
        rc[g, 0:8] = r[0:8]
        rc[g, 8:16] = r[nn - 8:nn]
    pe = np.zeros((128, 2, 16), np.float32)
    for p in range(128):
        for ch in range(2):
            pe[p, ch] = rc[ch * 2 + p // 64]
    c["c_pedge"] = pe
    return c


CONST_SHAPES = {
    "c_ident": ([128, 128], BF16), "c_onesdiv": ([128, 128], BF16), "c_ropeP": ([128, 128], BF16), "c_pm": ([128, 128], F32),
    "c_nm": ([128, 128], F32), "c_iota1": ([128, 128], F32), "c_iotar": ([128, 128], F32),
    "c_jcols": ([128, 2], F32), "c_cs64": ([128, 256], BF16), "c_dftctx": ([128, 2, 2, CTX], BF16),
    "c_pedge": ([128, 2, 16], F32), "c_hmask": ([128, 2, 128], BF16), "c_bmask": ([128, 128], F32),
}


class _Stop(Exception):
    pass


def build(depth=DEPTH, dbg=None, stop=None):
    nc = bass.Bass("TRN2", target_bir_lowering=False)

    def din(name, shape, dt=F32):
        return nc.dram_tensor(name, list(shape), dt, kind="ExternalInput").ap()

    xin = din("xin", [128, KC, NTOK])
    c2 = din("c2", [128, KC, 2])
    w_mod = din("w_mod", [DEPTH, D, 6 * D])
    b_mod = din("b_mod", [DEPTH, 128, 48])
    norm_g = din("norm_g", [DEPTH, 128, 4, KC])
    w_in = din("w_in", [DEPTH, D, 6400])
    ret_decay = din("ret_decay", [DEPTH, 8])
    conv_w = din("conv_w", [DEPTH, 128, 2, 3])
    pool_w = din("pool_w", [DEPTH, 4, 64, 64])
    pool_scale = din("pool_scale", [DEPTH, 128, 2])
    w_branch = din("w_branch", [DEPTH, 4, 256, D])
    w_o = din("w_o", [DEPTH, D, D])
    w_up = din("ffn_w_up", [DEPTH, D, 2 * DFF])
    w_down = din("ffn_w_down", [DEPTH, DFF, D])
    cd = {k: din(k, sh, dt) for k, (sh, dt) in CONST_SHAPES.items()}
    cos_d = din("c_cos", [128, SEQ], BF16)
    sin_d = din("c_sin", [128, SEQ], BF16)
    dftc = din("c_dftc", [SEQ, SEQ], BF16)
    dfts = din("c_dfts", [SEQ, SEQ], BF16)
    yout = nc.dram_tensor("yout", [128, KC, SEQ], F32, kind="ExternalOutput").ap()
    h_scr = nc.dram_tensor("h_scr", [128, KC, NTOK], BF16).ap()
    br_scr = nc.dram_tensor("br_scr", [128, 8, NTOK], BF16).ap()
    dbg_out = None
    if dbg is not None:
        dbg_out = nc.dram_tensor("dbg", list(dbg[1]), dbg[2], kind="ExternalOutput").ap()

    P = Prog(nc, n_epochs=depth)
    from contextlib import ExitStack
    top = ExitStack()

    uid = [0]

    def chk(tag):
        if stop == tag:
            raise _Stop()

    def sb(name, shape, dt, st=None):
        uid[0] += 1
        return (st or top).enter_context(nc.sbuf_tensor(f"{name}_u{uid[0]}", list(shape), dt))

    def MM(out, lhsT, rhs, start, stop, r, w):
        P.add("pe", lambda e: e.matmul(out, lhsT, rhs, start=start, stop=stop), reads=r, writes=w,
              nosig=not stop)

    def TR(out, in_, r, w):
        P.add("pe", lambda e: e.transpose(out, in_, ident[:]), reads=list(r) + ["c_ident"], writes=w)

    def ACT(out, in_, func, r, w, bias=None, scale=None):
        kw = {}
        if bias is not None:
            kw["bias"] = bias
        if scale is not None:
            kw["scale"] = scale
        P.add("act", lambda e: e.activation(out, in_, func, **kw), reads=r, writes=w)

    def TT(eng, out, in0, in1, op, r, w):
        P.add(eng, lambda e: e.tensor_tensor(out, in0, in1, op), reads=r, writes=w)

    def TS(eng, out, in0, s1, s2, op0, op1, r, w):
        if op1 is None:
            P.add(eng, lambda e: e.tensor_scalar(out, in0, s1, None, op0=op0), reads=r, writes=w)
        else:
            P.add(eng, lambda e: e.tensor_scalar(out, in0, s1, s2, op0=op0, op1=op1), reads=r, writes=w)

    def STT(eng, out, in0, scalar, in1, op0, op1, r, w):
        P.add(eng, lambda e: e.scalar_tensor_tensor(out, in0, scalar, in1, op0=op0, op1=op1), reads=r, writes=w)

    def CP(eng, out, in_, r, w):
        if eng == "act":
            P.add("act", lambda e: e.copy(out, in_), reads=r, writes=w)
        else:
            P.add(eng, lambda e: e.tensor_copy(out, in_), reads=r, writes=w)

    def MEMSET(eng, ap, val, w):
        P.add(eng, lambda e: e.memset(ap, val), writes=w)

    def DMA(q, out, in_, r, w):
        P.add(q, lambda e: e.dma_start(out=out, in_=in_), reads=r, writes=w, dma=True)

    class Ring:
        def __init__(self, name, shape, dt, n, st=None):
            self.t = [sb(f"{name}{i}", shape, dt, st) for i in range(n)]
            self.k = [f"{name}{i}" for i in range(n)]
            self.i = 0
            self.n = n

        def next(self):
            i = self.i
            self.i = (i + 1) % self.n
            return self.t[i], self.k[i]

    x_all = sb("x_all", [128, KC, NTOK], F32)
    cst = {k: sb("s" + k, sh, dt) for k, (sh, dt) in CONST_SHAPES.items()}
    ident = cst["c_ident"]
    onesdiv = cst["c_onesdiv"]
    epsc = sb("epsc", [128, 1], F32)
    cc2 = sb("cc2", [128, KC, 2], F32)
    scT = sb("scT", [128, KC, 2], BF16)
    modv = [sb(f"modv{i}", [128, 48, 2], F32) for i in range(2)]
    bmod = [sb(f"bmod{i}", [128, 48], F32) for i in range(2)]
    gnorm = [sb(f"gnorm{i}", [128, 4, KC], F32) for i in range(2)]
    A1 = [sb(f"A1_{i}", [128, KC, 2], F32) for i in range(2)]
    B1 = [sb(f"B1_{i}", [128, KC, 2], F32) for i in range(2)]
    A2 = [sb(f"A2_{i}", [128, KC, 2], F32) for i in range(2)]
    B2 = [sb(f"B2_{i}", [128, KC, 2], F32) for i in range(2)]
    cw = sb("cw", [128, 2, 3], F32)
    pscale = sb("pscale", [128, 2], F32)
    pwbd = sb("pwbd", [128, 2, 128], BF16)
    rd = sb("rdec", [128, 8], F32)
    lg = sb("lg", [128, 8], F32)
    lgsel = sb("lgsel", [128, 4], F32)
    m_all = sb("m_all", [128, 4, 128], BF16)
    kdt = sb("kdt", [128, 8], F32)
    qdt = sb("qdt", [128, 2, 2, 128], BF16)
    cdt = sb("cdt", [128, 4], F32)
    NSLOT = 4
    SLOTN = 4096
    wbig = sb("wbig", [128, NSLOT * SLOTN], BF16)
    wslot = [wbig[:, i * SLOTN:(i + 1) * SLOTN] for i in range(NSLOT)]
    wsl_i = [0]

    def next_slot():
        i = wsl_i[0]
        wsl_i[0] = (i + 1) % NSLOT
        return wslot[i], f"wslot{i}"

    ps = [top.enter_context(nc.psum_tensor(f"ps{i}", [128, 512], F32)) for i in range(8)]
    ps_i = [0]

    def PS():
        i = ps_i[0]
        ps_i[0] = (i + 1) % 8
        return ps[i], ("ps", i)

    tf = Ring("tf", [128, 512], F32, 3)
    tb = Ring("tb", [128, 512], BF16, 3)
    rst = Ring("rst", [128, 512], F32, 2)

    for k in CONST_SHAPES:
        DMA("sp", cst[k][:], cd[k], [], [k])
    MEMSET("dve", epsc[:], EPS, ["epsc"])
    MEMSET("dve", pwbd[:], 0.0, [("pwbd", g) for g in range(4)])
    DMA("sp", cc2[:], c2, [], ["cc2"])
    for (t0, T, s) in TILES:
        DMA("sp", x_all[:, :, t0:t0 + T], xin[:, :, t0:t0 + T], [], [("x", t0)])
    ACT(scT[:], cc2[:], AF.Silu, ["cc2"], ["scT"])

    def xkeys(t0, T):
        return [("x", t0)] if (t0 % 512 == 0 and (T == 512 or t0 == 2048)) else \
            [("x", a) for a in sorted({(t0 // 512) * 512, ((t0 + T - 1) // 512) * 512})]

    def emit_mod(l):
        pb = l % 2
        mv = modv[pb]
        DMA("sp", bmod[pb][:], b_mod[l], [], [f"bmod{pb}"])
        DMA("sp", gnorm[pb][:], norm_g[l], [], [f"gnorm{pb}"])
        pm, pmk = PS()
        wv = w_mod[l].rearrange("(k p) n -> p k n", p=128)
        for piece in range(12):
            ws, wk = next_slot()
            wsv = ws[:, 0:KC * 512].rearrange("p (k n) -> p k n", k=KC)
            DMA("pool", wsv, wv[:, :, piece * 512:(piece + 1) * 512], [], [wk])
            for jj in range(4):
                j = piece * 4 + jj
                for k in range(KC):
                    MM(pm[:, 2 * j:2 * j + 2], wsv[:, k, jj * 128:(jj + 1) * 128], scT[:, k, :],
                       k == 0, k == KC - 1, [wk, "scT"], [pmk])
        TT("dve", mv[:], pm[:, 0:96].rearrange("p (j s) -> p j s", s=2),
           bmod[pb][:].unsqueeze(2).broadcast_to([128, 48, 2]), ALU.add, [pmk, f"bmod{pb}"], [f"modv{pb}"])
        g = gnorm[pb]
        for (dst, nm, sc_lo, gi, plus1) in ((A1[pb], "A1", 8, 0, True), (B1[pb], "B1", 16, 1, False),
                                            (A2[pb], "A2", 32, 2, True), (B2[pb], "B2", 40, 3, False)):
            gb = g[:, gi, :].unsqueeze(2).broadcast_to([128, KC, 2])
            if plus1:
                STT("dve", dst[:], mv[:, sc_lo:sc_lo + KC, :], 1.0, gb, ALU.add, ALU.mult,
                    [f"modv{pb}", f"gnorm{pb}"], [f"{nm}_{pb}"])
            else:
                TT("dve", dst[:], mv[:, sc_lo:sc_lo + KC, :], gb, ALU.mult,
                   [f"modv{pb}", f"gnorm{pb}"], [f"{nm}_{pb}"])

    def emit_rstd(sq, sqk, T):
        pr, prk = PS()
        for k in range(KC):
            MM(pr[:, 0:T], onesdiv[:], sq[:, k, 0:T], k == 0, k == KC - 1, [sqk, "c_onesdiv"], [prk])
        t1, t1k = tf.next()
        ACT(t1[:, 0:T], pr[:, 0:T], AF.Sqrt, [prk, "epsc"], [t1k], bias=epsc[:, 0:1], scale=1.0)
        rs, rsk = rst.next()
        P.add("dve", lambda e: e.reciprocal(rs[:, 0:T], t1[:, 0:T]), reads=[t1k], writes=[rsk])
        return rs, rsk

    def emit_modulate(dst, dstk, dcol, t0, T, s, Asc, Ak, sh_lo, pb, rs, rsk):
        mv = modv[pb]
        for k in range(KC):
            t1, t1k = tf.next()
            STT("dve", t1[:, 0:T], x_all[:, k, t0:t0 + T], Asc[:, k, s:s + 1], rs[:, 0:T], ALU.mult, ALU.mult,
                xkeys(t0, T) + [Ak, rsk], [t1k])
            ACT(dst[:, k, dcol:dcol + T], t1[:, 0:T], AF.Identity, [t1k, f"modv{pb}"], [dstk],
                bias=mv[:, sh_lo + k, s:s + 1], scale=1.0)

    def emit_layer(l):
        pb = l % 2
        P.set_epoch(l)
        wl = w_in[l].rearrange("(k p) n -> p k n", p=128)
        DMA("sp", cw[:], conv_w[l], [], ["cw"])
        DMA("sp", pscale[:], pool_scale[l], [], ["pscale"])
        for g in range(4):
            DMA("pool", pwbd[(g % 2) * 64:(g % 2) * 64 + 64, g // 2, (g % 2) * 64:(g % 2) * 64 + 64],
                pool_w[l, g], [], [("pwbd", g)])
        DMA("sp", rd[:], ret_decay[l].partition_broadcast(128), [], ["rd"])
        ACT(lg[:], rd[:], AF.Exp, ["rd"], ["lg"], scale=-1.0)
        ACT(lg[:], lg[:], AF.Ln, ["lg"], ["lg"], bias=1.0)
        TS("dve", lg[:], lg[:], -1.0, None, ALU.mult, None, ["lg"], ["lg"])
        for dr in range(2):
            for pr_ in range(2):
                for hh in range(2):
                    CP("dve", lgsel[hh * 64:hh * 64 + 64, dr * 2 + pr_:dr * 2 + pr_ + 1],
                       lg[hh * 64:hh * 64 + 64, dr * 4 + pr_ * 2 + hh:dr * 4 + pr_ * 2 + hh + 1], ["lg"], ["lgsel"])
        for h in range(4):
            t1, t1k = tf.next()
            TS("dve", t1[:, 0:128], cst["c_pm"][:], lg[:, h:h + 1], None, ALU.mult, None, ["lg", "c_pm"], [t1k])
            STT("dve", t1[:, 128:256], cst["c_nm"][:], lg[:, 4 + h:5 + h], t1[:, 0:128], ALU.mult, ALU.add,
                ["lg", "c_nm", t1k], [t1k])
            ACT(m_all[:, h, :], t1[:, 128:256], AF.Exp, [t1k], ["m_all"])
        ACT(kdt[:, 0:4], lg[:, 0:4], AF.Exp, ["lg", "c_jcols"], ["kdt"], scale=cst["c_jcols"][:, 0:1])
        ACT(kdt[:, 4:8], lg[:, 4:8], AF.Exp, ["lg", "c_jcols"], ["kdt"], scale=cst["c_jcols"][:, 1:2])
        for dr in range(2):
            src = cst["c_iota1"] if dr == 0 else cst["c_iotar"]
            for pr_ in range(2):
                ACT(qdt[:, dr, pr_, :], src[:], AF.Exp, ["lgsel", "c_iota1", "c_iotar"], ["qdt"],
                    scale=lgsel[:, dr * 2 + pr_:dr * 2 + pr_ + 1])
        ACT(cdt[:], lgsel[:], AF.Exp, ["lgsel"], ["cdt"], scale=128.0)

        with ExitStack() as st:
            sqr = Ring("n1sq", [128, KC, 512], BF16, 2, st)
            hr = Ring("n1h", [128, KC, 512], BF16, 2, st)
            def n1_a(t0, T, s):
                sq, sqk = sqr.next()
                ACT(sq[:, :, 0:T], x_all[:, :, t0:t0 + T], AF.Square, xkeys(t0, T), [sqk])
                return emit_rstd(sq, sqk, T)

            def n1_b(t0, T, s, rs, rsk):
                hb, hk = hr.next()
                emit_modulate(hb, hk, 0, t0, T, s, A1[pb], f"A1_{pb}", 0, pb, rs, rsk)
                DMA("sp", h_scr[:, :, t0:t0 + T], hb[:, :, 0:T], [hk], [("hscr", t0)])
            pend = n1_a(*TILES[0])
            for ti_ in range(len(TILES)):
                nxt = n1_a(*TILES[ti_ + 1]) if ti_ + 1 < len(TILES) else None
                n1_b(*TILES[ti_], *pend)
                pend = nxt
            P.barrier()
        chk("N1")

        def load_h(hr, t0, T):
            hb, hk = hr.next()
            DMA("sp", hb[:, :, 0:T], h_scr[:, :, t0:t0 + T], [("hscr", t0)], [hk])
            return hb, hk

        with ExitStack() as st:
            hr = Ring("rh", [128, KC, 512], BF16, 1, st)
            wq = wbig[:, 0:KC * 1024].rearrange("p (k n) -> p k n", k=KC)
            RW = ["wslot0", "wslot1"]
            rcsr = Ring("rcs", [128, 2, 512], BF16, 2, st)
            qT = sb("qT", [128, 2, NTOK], BF16, st)
            kT = sb("kT", [128, 2, NTOK], BF16, st)
            sgT = sb("sgT", [128, 2, NTOK], BF16, st)
            vtok = sb("vtok", [128, NCH, 256], BF16, st)
            sball = sb("sball", [128, NCH, 2, 128], BF16, st)
            ktr = Ring("ktok", [128, 256], BF16, 2, st)
            vpr = Ring("vpr", [128, 256], BF16, 2, st)
            sst = [sb(f"sst{i}", [128, 2, 128], F32, st) for i in range(2)]
            sfb = Ring("sfb", [128, 2, 128], BF16, 2, st)
            scmr = Ring("scm", [128, 4, 128], BF16, 2, st)
            qpr = Ring("qpr", [128, 2, 2, 128], BF16, 2, st)
            qmr = Ring("qmr", [128, 2, 2, 128], BF16, 2, st)
            bmask3 = cst["c_bmask"][:].unsqueeze(1).broadcast_to([128, 2, 128])
            ysr = Ring("ysb", [128, 256], F32, 2, st)
            y2r = Ring("ysq", [128, 256], F32, 2, st)
            ynr = Ring("yn", [128, 256], BF16, 2, st)
            str_ = Ring("stat", [128, 16], F32, 2, st)
            brr = Ring("brr", [128, 2, 512], BF16, 2, st)
            DMA("pool", wq, wl[:, :, 0:1024], [], RW)
            for (t0, T, s) in TILES:
                hb, hk = load_h(hr, t0, T)
                if s == 0:
                    rcs, rcsk = rcsr.next()
                    DMA("sp", rcs[:, 0, :], cos_d[:, t0:t0 + T], [], [rcsk])
                    DMA("sp", rcs[:, 1, :], sin_d[:, t0:t0 + T], [], [rcsk + "s"])
                for j in range(4):
                    isq = j < 2
                    dst = qT if isq else kT
                    dk = "qT" if isq else "kT"
                    pp, ppk = PS()
                    for k in range(KC):
                        MM(pp[:, 0:T], wq[:, k, j * 128:(j + 1) * 128], hb[:, k, 0:T], k == 0, k == KC - 1,
                           RW + [hk], [ppk])
                    if s == 1:
                        ACT(dst[:, j % 2, t0:t0 + T], pp[:, 0:T], AF.Copy, [ppk], [(dk, t0)],
                            scale=(0.125 if isq else 1.0))
                    else:
                        qb, qbk = tb.next()
                        ACT(qb[:, 0:T], pp[:, 0:T], AF.Copy, [ppk], [qbk], scale=(0.125 if isq else 1.0))
                        p2, p2k = PS()
                        MM(p2[:, 0:T], cst["c_ropeP"][:], qb[:, 0:T], True, True, [qbk, "c_ropeP"], [p2k])
                        t1, t1k = tf.next()
                        TT("dve", t1[:, 0:T], qb[:, 0:T], rcs[:, 0, 0:T], ALU.mult, [qbk, rcsk], [t1k])
                        t2, t2k = tf.next()
                        TT("dve", t2[:, 0:T], p2[:, 0:T], rcs[:, 1, 0:T], ALU.mult, [p2k, rcsk + "s"], [t2k])
                        TT("pool", dst[:, j % 2, t0:t0 + T], t1[:, 0:T], t2[:, 0:T], ALU.add, [t1k, t2k], [(dk, t0)])
                for jj in range(2):
                    pp, ppk = PS()
                    for k in range(KC):
                        MM(pp[:, 0:T], wq[:, k, 768 + jj * 128:768 + (jj + 1) * 128], hb[:, k, 0:T], k == 0,
                           k == KC - 1, RW + [hk], [ppk])
                    ACT(sgT[:, jj, t0:t0 + T], pp[:, 0:T], AF.Silu, [ppk], [("sgT", t0)])
                for sub in range(T // 128):
                    c = t0 // 128 + sub
                    pp, ppk = PS()
                    for k in range(KC):
                        MM(pp[:, 0:256], hb[:, k, sub * 128:(sub + 1) * 128], wq[:, k, 512:768], k == 0, k == KC - 1,
                           RW + [hk], [ppk])
                    CP("act", vtok[:, c, :], pp[:, 0:256], [ppk], [("vtok", c)])

            def tile_of(c):
                return (c * 128 // 512) * 512 if c < 16 else 2048

            def k_transpose(c):
                pt, ptk = PS()
                ptb = pt[:].bitcast(BF16)
                for pr_ in range(2):
                    TR(ptb[:, pr_ * 128:(pr_ + 1) * 128], kT[:, pr_, c * 128:(c + 1) * 128], [("kT", tile_of(c))], [ptk])
                kt, ktk = ktr.next()
                CP("act", kt[:], ptb[:, 0:256], [ptk], [ktk])
                return kt, ktk

            def delta_state(c, dr, kt, ktk):
                vp, vpk = vpr.next()
                TT("dve", vp[:].rearrange("p (h d) -> p h d", h=4), vtok[:, c, :].rearrange("p (h d) -> p h d", h=4),
                   kdt[:, dr * 4:dr * 4 + 4].unsqueeze(2).broadcast_to([128, 4, 64]), ALU.mult,
                   [("vtok", c), "kdt"], [vpk])
                pd, pdk = PS()
                for pr_ in range(2):
                    MM(pd[:, pr_ * 128:(pr_ + 1) * 128], kt[:, pr_ * 128:(pr_ + 1) * 128],
                       vp[:, pr_ * 128:(pr_ + 1) * 128], True, True, [ktk, vpk], [pdk])
                return pd, pdk

            def update_state(dr, pd, pdk):
                for pr_ in range(2):
                    STT("dve", sst[dr][:, pr_, :], sst[dr][:, pr_, :], cdt[:, dr * 2 + pr_:dr * 2 + pr_ + 1],
                        pd[:, pr_ * 128:(pr_ + 1) * 128], ALU.mult, ALU.add, [f"sst{dr}", "cdt", pdk], [f"sst{dr}"])

            MEMSET("dve", sst[1][:], 0.0, ["sst1"])
            MEMSET("dve", sst[0][:], 0.0, ["sst0"])
            for c in [17, 16] + list(range(15, -1, -1)):
                TT("pool", sball[:, c, :, :], sst[1][:], bmask3, ALU.mult, ["sst1", "c_bmask"], [("sball", c)])
                kt, ktk = k_transpose(c)
                pd, pdk = delta_state(c, 1, kt, ktk)
                update_state(1, pd, pdk)
            order = [16, 17] + list(range(16))
            brst = {"br": None, "brk": None}

            def r2_a(c):
                tl = tile_of(c)
                sf, sfk = sfb.next()
                TT("pool", sf[:], sst[0][:], bmask3, ALU.mult, ["sst0", "c_bmask"], [sfk])
                qm, qmk = qmr.next()
                for pr_ in range(2):
                    TT("dve", qm[:, pr_, :, :], qT[:, pr_, c * 128:(c + 1) * 128].unsqueeze(1).broadcast_to([128, 2, 128]),
                       cst["c_hmask"][:], ALU.mult, [("qT", tl), "c_hmask"], [qmk])
                pscr, pscrk = PS()
                for pr_ in range(2):
                    MM(pscr[:, pr_ * 256:(pr_ + 1) * 256], kT[:, pr_, c * 128:(c + 1) * 128],
                       qm[:, pr_, :, :].rearrange("p a i -> p (a i)"), True, True, [("kT", tl), qmk], [pscrk])
                scm, scmk = scmr.next()
                TT("dve", scm[:], pscr[:].rearrange("p (h i) -> p h i", h=4), m_all[:], ALU.mult,
                   [pscrk, "m_all"], [scmk])
                qp, qpk = qpr.next()
                for dr in range(2):
                    TT("pool", qp[:, dr, :, :], qT[:, :, c * 128:(c + 1) * 128], qdt[:, dr, :, :], ALU.mult,
                       [("qT", tl), "qdt"], [qpk])
                py, pyk = PS()
                for pr_ in range(2):
                    MM(py[:, pr_ * 128:(pr_ + 1) * 128], qp[:, 0, pr_, :], sf[:, pr_, :], True, False, [qpk, sfk], [pyk])
                    MM(py[:, pr_ * 128:(pr_ + 1) * 128], qp[:, 1, pr_, :], sball[:, c, pr_, :], False, False,
                       [qpk, ("sball", c)], [pyk])
                    for hh in range(2):
                        h = pr_ * 2 + hh
                        MM(py[:, h * 64:(h + 1) * 64], scm[:, h, :], vtok[:, c, h * 64:(h + 1) * 64], False, hh == 1,
                           [scmk, ("vtok", c)], [pyk])
                kt, ktk = k_transpose(c)
                pd, pdk = delta_state(c, 0, kt, ktk)
                update_state(0, pd, pdk)
                ys, ysk = ysr.next()
                CP("act", ys[:], py[:, 0:256], [pyk], [ysk])
                y2, y2k = y2r.next()
                ACT(y2[:], py[:, 0:256], AF.Square, [pyk], [y2k])
                return ys, ysk, y2, y2k

            def r2_b(c, ys, ysk, y2, y2k):
                tl = tile_of(c)
                br, brk = brst["br"], brst["brk"]
                stt, stk = str_.next()
                P.add("dve", lambda e, stt=stt, ys=ys: e.reduce_sum(stt[:, 0:4], ys[:].rearrange("p (h d) -> p h d", h=4),
                                                                    axis=AX.X), reads=[ysk], writes=[stk])
                P.add("dve", lambda e, stt=stt, y2=y2: e.reduce_sum(stt[:, 4:8], y2[:].rearrange("p (h d) -> p h d", h=4),
                                                                    axis=AX.X), reads=[y2k, stk], writes=[stk])
                TS("dve", stt[:, 0:4], stt[:, 0:4], 1.0 / 64, None, ALU.mult, None, [stk], [stk])
                TT("dve", stt[:, 8:12], stt[:, 0:4], stt[:, 0:4], ALU.mult, [stk], [stk])
                STT("dve", stt[:, 4:8], stt[:, 4:8], 1.0 / 64, stt[:, 8:12], ALU.mult, ALU.subtract, [stk], [stk])
                ACT(stt[:, 8:12], stt[:, 4:8], AF.Sqrt, [stk, "epsc"], [stk], bias=epsc[:, 0:1], scale=1.0)
                P.add("dve", lambda e, stt=stt: e.reciprocal(stt[:, 12:16], stt[:, 8:12]), reads=[stk], writes=[stk])
                TT("dve", ys[:].rearrange("p (h d) -> p h d", h=4), ys[:].rearrange("p (h d) -> p h d", h=4),
                   stt[:, 0:4].unsqueeze(2).broadcast_to([128, 4, 64]), ALU.subtract, [ysk, stk], [ysk])
                yn, ynk = ynr.next()
                TT("dve", yn[:].rearrange("p (h d) -> p h d", h=4), ys[:].rearrange("p (h d) -> p h d", h=4),
                   stt[:, 12:16].unsqueeze(2).broadcast_to([128, 4, 64]), ALU.mult, [ysk, stk], [ynk])
                pt, ptk = PS()
                ptb = pt[:].bitcast(BF16)
                for pr_ in range(2):
                    TR(ptb[:, pr_ * 128:(pr_ + 1) * 128], yn[:, pr_ * 128:(pr_ + 1) * 128], [ynk], [ptk])
                sub = (c * 128 - tl) // 128
                if sub == 0:
                    br, brk = brr.next()
                    brst["br"], brst["brk"] = br, brk
                TT("dve", br[:, :, sub * 128:(sub + 1) * 128], ptb[:, 0:256].rearrange("p (a t) -> p a t", a=2),
                   sgT[:, :, c * 128:(c + 1) * 128], ALU.mult, [ptk, ("sgT", tl)], [brk])
                last = (sub == 3) if c < 16 else (sub == 1)
                if last:
                    T = 512 if c < 16 else 256
                    DMA("sp", br_scr[:, 0:2, tl:tl + T], br[:, :, 0:T], [brk], [("brscr", 0, tl)])

            pend = r2_a(order[0])
            for oi, c in enumerate(order):
                nxt = r2_a(order[oi + 1]) if oi + 1 < len(order) else None
                r2_b(c, *pend)
                pend = nxt
            P.barrier()
        chk("R")

        with ExitStack() as st:
            hr = Ring("ch", [128, KC, 512], BF16, 1, st)
            wc = wbig[:, 0:KC * 768].rearrange("p (k n) -> p k n", k=KC)
            CW = ["wslot0", "wslot1"]
            convB = sb("convB", [128, 2, NTOK], BF16, st)
            convu = sb("convu", [128, 2, NTOK + 4], BF16, st)
            brr = Ring("cbr", [128, 2, 512], BF16, 2, st)
            DMA("pool", wc, wl[:, :, 1024:1792], [], CW)
            MEMSET("pool", convu[:], 0.0, ["convu"])

            def cpos(t0):
                return t0 + 1 if t0 < SEQ else t0 + 3
            for (t0, T, s) in TILES:
                hb, hk = load_h(hr, t0, T)
                for jj in range(2):
                    pp, ppk = PS()
                    for k in range(KC):
                        MM(pp[:, 0:T], wc[:, k, jj * 128:(jj + 1) * 128], hb[:, k, 0:T], k == 0, k == KC - 1,
                           CW + [hk], [ppk])
                    CP("act", convB[:, jj, t0:t0 + T], pp[:, 0:T], [ppk], [("convB", t0)])
                    pc, pck = PS()
                    for k in range(KC):
                        MM(pc[:, 0:T], wc[:, k, 256 + jj * 128:256 + (jj + 1) * 128], hb[:, k, 0:T], k == 0, k == KC - 1,
                           CW + [hk], [pck])
                    px, pxk = PS()
                    for k in range(KC):
                        MM(px[:, 0:T], wc[:, k, 512 + jj * 128:512 + (jj + 1) * 128], hb[:, k, 0:T], k == 0, k == KC - 1,
                           CW + [hk], [pxk])
                    t1, t1k = tf.next()
                    CP("act", t1[:, 0:T], pc[:, 0:T], [pck], [t1k])
                    p0 = cpos(t0)
                    TT("dve", convu[:, jj, p0:p0 + T], px[:, 0:T], t1[:, 0:T], ALU.mult, [pxk, t1k], ["convu"])
            for (t0, T, s) in TILES:
                br, brk = brr.next()
                p0 = cpos(t0)
                for jj in range(2):
                    t1, t1k = tf.next()
                    TS("dve", t1[:, 0:T], convu[:, jj, p0 - 1:p0 - 1 + T], cw[:, jj, 0:1], None, ALU.mult, None,
                       ["convu", "cw"], [t1k])
                    t2, t2k = tf.next()
                    STT("dve", t2[:, 0:T], convu[:, jj, p0:p0 + T], cw[:, jj, 1:2], t1[:, 0:T], ALU.mult, ALU.add,
                        ["convu", "cw", t1k], [t2k])
                    t3, t3k = tf.next()
                    STT("dve", t3[:, 0:T], convu[:, jj, p0 + 1:p0 + 1 + T], cw[:, jj, 2:3], t2[:, 0:T], ALU.mult, ALU.add,
                        ["convu", "cw", t2k], [t3k])
                    TT("pool", br[:, jj, 0:T], t3[:, 0:T], convB[:, jj, t0:t0 + T], ALU.mult, [t3k, ("convB", t0)], [brk])
                DMA("sp", br_scr[:, 2:4, t0:t0 + T], br[:, :, 0:T], [brk], [("brscr", 1, t0)])
            P.barrier()
        chk("C")

        with ExitStack() as st:
            hr = Ring("fh", [128, KC, 512], BF16, 1, st)
            wf = wbig[:, 0:KC * 256].rearrange("p (k n) -> p k n", k=KC)
            ucs = sb("ucs", [128, NCH, 2, 256], BF16, st)
            utr = Ring("fu", [128, 2, 512], BF16, 2, st)
            tabc = Ring("ftc", [128, 16, 256], BF16, 2, st)
            tabs = Ring("fts", [128, 16, 256], BF16, 2, st)
            brr = Ring("fbr", [128, 2, 512], BF16, 2, st)
            DMA("pool", wf, wl[:, :, 1792:2048], [], ["wslot0"])
            for (t0, T, s) in TILES:
                hb, hk = load_h(hr, t0, T)
                ut, utk = utr.next()
                for ch in range(2):
                    pp, ppk = PS()
                    for k in range(KC):
                        MM(pp[:, 0:T], wf[:, k, ch * 128:(ch + 1) * 128], hb[:, k, 0:T], k == 0, k == KC - 1,
                           ["wslot0", hk], [ppk])
                    CP("act", ut[:, ch, 0:T], pp[:, 0:T], [ppk], [utk])
                for sub in range(T // 128):
                    c = t0 // 128 + sub
                    pp, ppk = PS()
                    for ch in range(2):
                        MM(pp[:, ch * 256:(ch + 1) * 256], ut[:, ch, sub * 128:(sub + 1) * 128], cst["c_cs64"][:],
                           True, True, [utk, "c_cs64"], [ppk])
                    CP("dve", ucs[:, c, :, :], pp[:].rearrange("p (a n) -> p a n", a=2), [ppk], [("ucs", c)])
            dcv = dftc.rearrange("(n p) k -> p n k", p=128)
            dsv = dfts.rearrange("(n p) k -> p n k", p=128)
            sc_lat = 1.0 / float(np.sqrt(SEQ * 64.0))
            tabq = []
            for kt_ in range(8):
                if kt_ == 0:
                    for pf in range(2):
                        tc_, tck = tabc.next()
                        ts_, tsk = tabs.next()
                        DMA("sp", tc_[:], dcv[:, :, pf * 256:(pf + 1) * 256], [], [tck])
                        DMA("sp", ts_[:], dsv[:, :, pf * 256:(pf + 1) * 256], [], [tsk])
                        tabq.append((tc_, tck, ts_, tsk))
                tc_, tck, ts_, tsk = tabq[kt_]
                if kt_ % 2 == 0:
                    br, brk = brr.next()
                for ch in range(2):
                    pp, ppk = PS()
                    for n in range(16):
                        MM(pp[:, 0:256], ucs[:, n, ch, 0:128], tc_[:, n, :], n == 0, False, [("ucs", n), tck], [ppk])
                        MM(pp[:, 0:256], ucs[:, n, ch, 128:256], ts_[:, n, :], False, n == 15, [("ucs", n), tsk], [ppk])
                    ACT(br[:, ch, (kt_ % 2) * 256:(kt_ % 2 + 1) * 256], pp[:, 0:256], AF.Copy, [ppk], [brk], scale=sc_lat)
                if kt_ + 2 < 8:
                    tc2, tck2 = tabc.next()
                    ts2, tsk2 = tabs.next()
                    DMA("sp", tc2[:], dcv[:, :, (kt_ + 2) * 256:(kt_ + 3) * 256], [], [tck2])
                    DMA("sp", ts2[:], dsv[:, :, (kt_ + 2) * 256:(kt_ + 3) * 256], [], [tsk2])
                    tabq.append((tc2, tck2, ts2, tsk2))
                if kt_ % 2 == 1:
                    k0 = (kt_ // 2) * 512
                    DMA("sp", br_scr[:, 4:6, k0:k0 + 512], br[:], [brk], [("brscr", 2, k0)])
            br, brk = brr.next()
            dx = cst["c_dftctx"]
            for ch in range(2):
                pp, ppk = PS()
                for n in range(2):
                    MM(pp[:, 0:CTX], ucs[:, 16 + n, ch, 0:128], dx[:, 0, n, :], n == 0, False,
                       [("ucs", 16 + n), "c_dftctx"], [ppk])
                    MM(pp[:, 0:CTX], ucs[:, 16 + n, ch, 128:256], dx[:, 1, n, :], False, n == 1,
                       [("ucs", 16 + n), "c_dftctx"], [ppk])
                ACT(br[:, ch, 0:CTX], pp[:, 0:CTX], AF.Copy, [ppk], [brk], scale=1.0 / 128.0)
            DMA("sp", br_scr[:, 4:6, SEQ:NTOK], br[:, :, 0:CTX], [brk], [("brscr", 2, SEQ)])
            P.barrier()
        chk("F")

        with ExitStack() as st:
            hr = Ring("ph", [128, KC, 512], BF16, 1, st)
            wp = wbig[:, 0:KC * 256].rearrange("p (k n) -> p k n", k=KC)
            PW = NTOK + 64
            pu = sb("poolu", [128, 2, PW], BF16, st)
            a2 = sb("pa2", [128, 2, 528], F32, st)
            a4 = sb("pa4", [128, 2, 528], F32, st)
            a8 = sb("pa8", [128, 2, 528], F32, st)
            a16 = sb("pa16", [128, 2, 528], F32, st)
            pld = Ring("pld", [128, 2, 512], BF16, 2, st)
            brr = Ring("pbr", [128, 2, 512], BF16, 2, st)
            ped = Ring("ped", [128, 16], F32, 2, st)
            DMA("pool", wp, wl[:, :, 2048:2304], [], ["wslot0"])
            MEMSET("pool", pu[:], 0.0, ["poolu"])

            def ppos(t0):
                return t0 + 16 if t0 < SEQ else t0 + 48
            for (t0, T, s) in TILES:
                hb, hk = load_h(hr, t0, T)
                for ch in range(2):
                    pp, ppk = PS()
                    for k in range(KC):
                        MM(pp[:, 0:T], wp[:, k, ch * 128:(ch + 1) * 128], hb[:, k, 0:T], k == 0, k == KC - 1,
                           ["wslot0", hk], [ppk])
                    p0 = ppos(t0)
                    CP("act", pu[:, ch, p0:p0 + T], pp[:, 0:T], [ppk], ["poolu"])
            for (t0, T, s) in TILES:
                p0 = ppos(t0)
                L = T + 16
                b0 = p0 - 8
                TT("dve", a2[:, :, 1:L], pu[:, :, b0 + 1:b0 + L], pu[:, :, b0:b0 + L - 1], ALU.add, ["poolu"], ["pa2"])
                TT("pool", a4[:, :, 3:L], a2[:, :, 3:L], a2[:, :, 1:L - 2], ALU.add, ["pa2"], ["pa4"])
                TT("dve", a8[:, :, 7:L], a4[:, :, 7:L], a4[:, :, 3:L - 4], ALU.add, ["pa4"], ["pa8"])
                TT("pool", a16[:, :, 15:L], a8[:, :, 15:L], a8[:, :, 7:L - 8], ALU.add, ["pa8"], ["pa16"])
                pl, plk = pld.next()
                wins = [(a2, "pa2", 0, 2), (a4, "pa4", 1, 4), (a8, "pa8", 3, 8), (a16, "pa16", 7, 16)]
                for g in range(4):
                    ch, hh = g // 2, g % 2
                    aw, awk, sh, w = wins[g]
                    lo = hh * 64
                    STT("dve", pl[lo:lo + 64, ch, 0:T], aw[lo:lo + 64, ch, 8 + sh:8 + sh + T], 1.0 / w,
                        pu[lo:lo + 64, ch, p0:p0 + T], ALU.mult, ALU.subtract, [awk, "poolu"], [plk])
                    edges = []
                    if t0 == 0 or t0 == SEQ:
                        edges.append((0, 0))
                    if t0 + T == SEQ or t0 + T == NTOK:
                        edges.append((T - 8, 8))
                    for (e0, r0) in edges:
                        pe_, pek = ped.next()
                        TT("dve", pe_[lo:lo + 64, 0:8], aw[lo:lo + 64, ch, 8 + sh + e0:8 + sh + e0 + 8],
                           cst["c_pedge"][lo:lo + 64, ch, r0:r0 + 8], ALU.mult, [awk, "c_pedge"], [pek])
                        TT("dve", pl[lo:lo + 64, ch, e0:e0 + 8], pe_[lo:lo + 64, 0:8],
                           pu[lo:lo + 64, ch, p0 + e0:p0 + e0 + 8], ALU.subtract, [pek, "poolu", plk], [plk])
                br, brk = brr.next()
                for ch in range(2):
                    pp, ppk = PS()
                    MM(pp[:, 0:T], pwbd[:, ch, :], pl[:, ch, 0:T], True, True, [("pwbd", 2 * ch), ("pwbd", 2 * ch + 1), plk], [ppk])
                    ACT(br[:, ch, 0:T], pp[:, 0:T], AF.Copy, [ppk, "pscale"], [brk], scale=pscale[:, ch:ch + 1])
                DMA("sp", br_scr[:, 6:8, t0:t0 + T], br[:, :, 0:T], [brk], [("brscr", 3, t0)])
            P.barrier()
        chk("P")

        if l + 1 < depth:
            emit_mod(l + 1)

        last_layer = (l == DEPTH - 1)
        with ExitStack() as st:
            A = sb("s_A", [128, KC, 768], F32, st)
            Bb = sb("s_B", [128, KC, 768], BF16, st)
            merged = sb("s_M", [128, KC, 768], BF16, st)
            hid = sb("s_H", [128, 11, 768], BF16, st)
            wbr = Ring("s_wb", [128, 2, 512], BF16, 4, st)
            steps = []

            def add_step(nslots, load, compute):
                steps.append({"n": nslots, "load": load, "compute": compute})

            def slot_view(ws, k, n):
                return ws[:, 0:k * n].rearrange("p (k n) -> p k n", k=k)

            for st_tiles0 in SUPER:
                st_tiles = [t for t in st_tiles0 if t[2] == 0] if last_layer else list(st_tiles0)
                offs = []
                o = 0
                for (t0, T, s) in st_tiles:
                    offs.append(o)
                    o += T
                tl_ = list(zip(range(len(st_tiles)), st_tiles, offs))

                def s0_load(tl_=tl_):
                    for ti, (t0, T, s), o_ in tl_:
                        hk_ = [("hscr", a) for a in ([(t0 // 512) * 512, ((t0 + T - 1) // 512) * 512] if t0 < SEQ else [SEQ])]
                        DMA("sp", Bb[:, :, o_:o_ + T], h_scr[:, :, t0:t0 + T], hk_, [("B", ti)])
                        bk_ = [("brscr", b_, a) for b_ in range(4)
                               for a in ([(t0 // 512) * 512, ((t0 + T - 1) // 512) * 512] if t0 < SEQ else [SEQ])]
                        DMA("sp", hid[:, 0:8, o_:o_ + T], br_scr[:, :, t0:t0 + T], bk_, [("H", ti)])
                add_step(0, lambda: None, s0_load)

                for i in range(4):
                    for half in range(2):
                        d = {}

                        def g_load(d=d, i=i, half=half):
                            ws, wk = next_slot()
                            d["wg"], d["wk"] = slot_view(ws, KC, 512), wk
                            c0 = MIXW + i * D + half * 512
                            DMA("pool", d["wg"], wl[:, :, c0:c0 + 512], [], [wk])
                            d["wb"], d["wbk"] = wbr.next()
                            DMA("pool", d["wb"][:], w_branch[l, i].rearrange("(k p) n -> p k n", p=128)[:, :, half * 512:(half + 1) * 512],
                                [], [d["wbk"]])

                        def g_comp(d=d, i=i, half=half, tl_=tl_):
                            wg, wk, wb, wbk = d["wg"], d["wk"], d["wb"], d["wbk"]
                            for c4 in range(4):
                                c = half * 4 + c4
                                for ti, (t0, T, s), o_ in tl_:
                                    pg, pgk = PS()
                                    for k in range(KC):
                                        MM(pg[:, 0:T], wg[:, k, c4 * 128:(c4 + 1) * 128], Bb[:, k, o_:o_ + T], k == 0,
                                           k == KC - 1, [wk, ("B", ti)], [pgk])
                                    pbp, pbk = PS()
                                    for kc in range(2):
                                        MM(pbp[:, 0:T], wb[:, kc, c4 * 128:(c4 + 1) * 128], hid[:, i * 2 + kc, o_:o_ + T],
                                           kc == 0, kc == 1, [wbk, ("H", ti)], [pbk])
                                    sg, sgk = tb.next()
                                    ACT(sg[:, 0:T], pg[:, 0:T], AF.Sigmoid, [pgk], [sgk])
                                    ak = ("A", c, ti)
                                    if i == 0:
                                        TT("dve", A[:, c, o_:o_ + T], pbp[:, 0:T], sg[:, 0:T], ALU.mult, [pbk, sgk], [ak])
                                    else:
                                        t1, t1k = tf.next()
                                        TT("dve", t1[:, 0:T], pbp[:, 0:T], sg[:, 0:T], ALU.mult, [pbk, sgk], [t1k])
                                        if i < 3:
                                            TT("pool", A[:, c, o_:o_ + T], A[:, c, o_:o_ + T], t1[:, 0:T], ALU.add, [ak, t1k], [ak])
                                        else:
                                            TT("pool", merged[:, c, o_:o_ + T], A[:, c, o_:o_ + T], t1[:, 0:T], ALU.add,
                                               [ak, t1k], [("M", ti)])
                        add_step(1, g_load, g_comp)

                for half in range(2):
                    d = {}

                    def o_load(d=d, half=half):
                        ws, wk = next_slot()
                        d["w"], d["wk"] = slot_view(ws, KC, 512), wk
                        DMA("pool", d["w"], w_o[l].rearrange("(k p) n -> p k n", p=128)[:, :, half * 512:(half + 1) * 512], [], [wk])

                    def o_comp(d=d, half=half, tl_=tl_):
                        wo, wk = d["w"], d["wk"]
                        for c4 in range(4):
                            c = half * 4 + c4
                            for ti, (t0, T, s), o_ in tl_:
                                pp, ppk = PS()
                                for k in range(KC):
                                    MM(pp[:, 0:T], wo[:, k, c4 * 128:(c4 + 1) * 128], merged[:, k, o_:o_ + T], k == 0,
                                       k == KC - 1, [wk, ("M", ti)], [ppk])
                                CP("act", A[:, c, o_:o_ + T], pp[:, 0:T], [ppk], [("A", c, ti)])
                                ACT(Bb[:, c, o_:o_ + T], pp[:, 0:T], AF.Square, [ppk], [("B", ti)])
                    add_step(1, o_load, o_comp)

                def resid(tl_, Bsc, Bk, use_sq_of):
                    for ti, (t0, T, s), o_ in tl_:
                        if use_sq_of == "B":
                            rs, rsk = emit_rstd(Bb[:, :, o_:o_ + T], ("B", ti), T)
                        else:
                            ACT(merged[:, :, o_:o_ + T], A[:, :, o_:o_ + T], AF.Square, [("A", c, ti) for c in range(KC)], [("M", ti)])
                            rs, rsk = emit_rstd(merged[:, :, o_:o_ + T], ("M", ti), T)
                        for c in range(KC):
                            t1, t1k = tf.next()
                            STT("dve", t1[:, 0:T], A[:, c, o_:o_ + T], Bsc[:, c, s:s + 1], rs[:, 0:T], ALU.mult, ALU.mult,
                                [("A", c, ti), Bk, rsk], [t1k])
                            TT("pool", x_all[:, c, t0:t0 + T], x_all[:, c, t0:t0 + T], t1[:, 0:T], ALU.add,
                               xkeys(t0, T) + [t1k], xkeys(t0, T))

                add_step(0, lambda: None, lambda tl_=tl_: resid(tl_, B1[pb], f"B1_{pb}", "B"))

                def n2(tl_=tl_):
                    for ti, (t0, T, s), o_ in tl_:
                        ACT(merged[:, :, o_:o_ + T], x_all[:, :, t0:t0 + T], AF.Square, xkeys(t0, T), [("M", ti)])
                        rs, rsk = emit_rstd(merged[:, :, o_:o_ + T], ("M", ti), T)
                        emit_modulate(Bb, ("B", ti), o_, t0, T, s, A2[pb], f"A2_{pb}", 24, pb, rs, rsk)
                add_step(0, lambda: None, n2)
                wu = w_up[l].rearrange("(k p) n -> p k n", p=128)
                for half in range(2):
                    for blk in range(3):
                        ncb = 4 if blk < 2 else 3
                        d = {}

                        def u_load(d=d, half=half, blk=blk, ncb=ncb):
                            a0 = half * 1408 + blk * 512
                            wsa, d["wak"] = next_slot()
                            d["wa"] = slot_view(wsa, KC, 512)
                            DMA("pool", d["wa"][:, :, 0:ncb * 128], wu[:, :, a0:a0 + ncb * 128], [], [d["wak"]])
                            wsu, d["wuk"] = next_slot()
                            d["wu"] = slot_view(wsu, KC, 512)
                            DMA("pool", d["wu"][:, :, 0:ncb * 128], wu[:, :, DFF + a0:DFF + a0 + ncb * 128], [], [d["wuk"]])

                        def u_comp(d=d, blk=blk, ncb=ncb, tl_=tl_):
                            wa, wak, wuu, wuk = d["wa"], d["wak"], d["wu"], d["wuk"]
                            for cb in range(ncb):
                                jj = blk * 4 + cb
                                for ti, (t0, T, s), o_ in tl_:
                                    pa, pak = PS()
                                    for k in range(KC):
                                        MM(pa[:, 0:T], wa[:, k, cb * 128:(cb + 1) * 128], Bb[:, k, o_:o_ + T], k == 0, k == KC - 1,
                                           [wak, ("B", ti)], [pak])
                                    pu_, puk = PS()
                                    for k in range(KC):
                                        MM(pu_[:, 0:T], wuu[:, k, cb * 128:(cb + 1) * 128], Bb[:, k, o_:o_ + T], k == 0, k == KC - 1,
                                           [wuk, ("B", ti)], [puk])
                                    sa, sak = tb.next()
                                    ACT(sa[:, 0:T], pa[:, 0:T], AF.Silu, [pak], [sak])
                                    TT("dve", hid[:, jj, o_:o_ + T], pu_[:, 0:T], sa[:, 0:T], ALU.mult, [puk, sak], [("H", ti)])
                        add_step(2, u_load, u_comp)
                    for cb2 in range(4):
                        d = {}

                        def d_load(d=d, half=half, cb2=cb2):
                            wdv = w_down[l, half * 1408:(half + 1) * 1408, :].rearrange("(j p) n -> p j n", p=128)
                            ws, d["wk"] = next_slot()
                            d["w"] = slot_view(ws, 11, 256)
                            DMA("pool", d["w"], wdv[:, :, cb2 * 256:(cb2 + 1) * 256], [], [d["wk"]])

                        def d_comp(d=d, half=half, cb2=cb2, tl_=tl_):
                            wd, wk = d["w"], d["wk"]
                            for c2 in range(2):
                                c = cb2 * 2 + c2
                                for ti, (t0, T, s), o_ in tl_:
                                    pp, ppk = PS()
                                    for jj in range(11):
                                        MM(pp[:, 0:T], wd[:, jj, c2 * 128:(c2 + 1) * 128], hid[:, jj, o_:o_ + T], jj == 0, jj == 10,
                                           [wk, ("H", ti)], [ppk])
                                    fk = ("A", c, ti)
                                    if half == 0:
                                        CP("act", A[:, c, o_:o_ + T], pp[:, 0:T], [ppk], [fk])
                                    else:
                                        TT("dve", A[:, c, o_:o_ + T], A[:, c, o_:o_ + T], pp[:, 0:T], ALU.add, [fk, ppk], [fk])
                        add_step(1, d_load, d_comp)
                add_step(0, lambda: None, lambda tl_=tl_: resid(tl_, B2[pb], f"B2_{pb}", "A"))

            loaded = 0
            for n, stp in enumerate(steps):
                while loaded < len(steps):
                    need = sum(s_["n"] for s_ in steps[n:loaded + 1])
                    if loaded <= n or need <= NSLOT:
                        steps[loaded]["load"]()
                        loaded += 1
                    else:
                        break
                stp["compute"]()
            P.barrier()

    emit_mod(0)
    try:
        chk("M")
        for l in range(depth):
            emit_layer(l)
    except _Stop:
        pass
    for (t0, T, s) in TILES:
        if s == 0:
            DMA("sp", yout[:, :, t0:t0 + T], x_all[:, :, t0:t0 + T], xkeys(t0, T), [("yout", t0)])
    if dbg is not None:
        dbg[0](nc, P, DMA, dbg_out, locals())
    P.finish("sp")
    top.close()
    return nc, P


_CONSTS = None


def _prep_inputs(x, c, ctx, c_ctx, w_mod, b_mod, norm_g, w_in, ret_decay, conv_w, pool_w, pool_scale,
                 w_branch, w_o, ffn_w_up, ffn_w_down):
    global _CONSTS
    if _CONSTS is None:
        _CONSTS = make_consts()
    f = lambda a: np.ascontiguousarray(np.asarray(a, dtype=np.float32))
    x, c, ctx, c_ctx = f(x), f(c), f(ctx), f(c_ctx)
    shared = {
        "w_mod": f(w_mod),
        "b_mod": np.ascontiguousarray(f(b_mod).reshape(DEPTH, 48, 128).transpose(0, 2, 1)),
        "norm_g": np.ascontiguousarray(f(norm_g).reshape(DEPTH, 4, KC, 128).transpose(0, 3, 1, 2)),
        "w_in": f(w_in),
        "ret_decay": f(ret_decay).reshape(DEPTH, 8),
        "conv_w": np.ascontiguousarray(f(conv_w).reshape(DEPTH, 3, 2, 128).transpose(0, 3, 2, 1)),
        "pool_w": f(pool_w),
        "pool_scale": np.ascontiguousarray(f(pool_scale).reshape(DEPTH, 2, 128).transpose(0, 2, 1)),
        "w_branch": f(w_branch),
        "w_o": f(w_o),
        "ffn_w_up": f(ffn_w_up),
        "ffn_w_down": f(ffn_w_down),
    }
    shared.update(_CONSTS)
    in_maps = []
    for b in range(x.shape[0]):
        xa = np.concatenate([x[b].T, ctx[b].T], axis=1)
        xa = np.ascontiguousarray(xa.reshape(KC, 128, NTOK).transpose(1, 0, 2))
        cc = np.stack([c[b], c_ctx], axis=1)
        cc = np.ascontiguousarray(cc.reshape(KC, 128, 2).transpose(1, 0, 2))
        m = dict(shared)
        m["xin"] = xa
        m["c2"] = cc
        in_maps.append(m)
    return in_maps


_NC = {}


def kernel(**inputs):
    in_maps = _prep_inputs(**inputs)
    if "nc" not in _NC:
        _NC["nc"] = build(DEPTH)[0]
    nc = _NC["nc"]
    res = run_bass_kernel_spmd(nc, in_maps, core_ids=list(range(len(in_maps))))
    outs = []
    for r in res.results:
        y = np.asarray(r["yout"], dtype=np.float32)
        outs.append(y.transpose(1, 0, 2).reshape(D, SEQ).T)
    return np.ascontiguousarray(np.stack(outs, axis=0))
```
